# Optimizing a Trainium2 kernel written in Bass

```python
import jax, jax.numpy as jnp
from jax import lax
import numpy as np

D_MODEL = 2048
BATCH = 4
SEQ = 4096
DEPTH = 4
DEC_BATCH = 16
DEC_SEQ = 16
PAST_LEN = 2048

CHUNK = 64
N_A = DEPTH // 2
N_B = DEPTH - N_A
EXPAND_A = 2
E_A = EXPAND_A * D_MODEL
POOL_WINDOWS = (2, 4, 8, 16)
N_POOL_GROUPS = len(POOL_WINDOWS)
G_A = E_A // N_POOL_GROUPS
POOL_PAD = max(POOL_WINDOWS) - 1
N_HEADS = 16
HEAD_DIM = D_MODEL // N_HEADS
W_B = N_HEADS * HEAD_DIM
Q_BLOCK = 128
EPS = 1e-6
F_BIAS_MEAN = 3.0

kernel_name = 'yoco_pool_fox_streaming_step'


def rmsnorm(x, g):
    xf = x.astype(jnp.float32)
    r = lax.rsqrt(jnp.mean(xf * xf, axis=-1, keepdims=True) + EPS)
    return (xf * r).astype(x.dtype) * g


def pool_mix(u, prev, pos0, w_grp, scale):
    B, T, _ = u.shape
    full = jnp.concatenate([prev.astype(u.dtype), u], axis=1)
    cs = jnp.cumsum(full.astype(jnp.float32), axis=1)
    cs = jnp.pad(cs, ((0, 0), (1, 0), (0, 0)))
    end = cs[:, POOL_PAD + 1:]
    uf = u.astype(jnp.float32)
    pos = pos0 + jnp.arange(T)
    outs = []
    for g, w in enumerate(POOL_WINDOWS):
        sl = slice(g * G_A, (g + 1) * G_A)
        start = lax.slice_in_dim(cs, POOL_PAD + 1 - w, POOL_PAD + 1 - w + T, axis=1)[..., sl]
        cnt = jnp.minimum(pos + 1, w).astype(jnp.float32)
        d = (end[..., sl] - start) / cnt[None, :, None] - uf[..., sl]
        outs.append(jnp.einsum('btc,cd->btd', d, w_grp[g].astype(jnp.float32)))
    y = jnp.concatenate(outs, axis=-1) * scale.astype(jnp.float32)
    return y.astype(u.dtype), full[:, -POOL_PAD:]


def fox_attention(q, k, v, Fq, Fk, q_pos, k_pos):
    B, T, H, Dh = q.shape
    scale = HEAD_DIM ** -0.5
    Fk_t = jnp.transpose(Fk, (0, 2, 1))

    def block(args):
        qb, Fqb, pb = args
        s = jnp.einsum('bqhd,bkhd->bhqk', qb, k, preferred_element_type=jnp.float32) * scale
        s = s + jnp.transpose(Fqb, (0, 2, 1))[..., None] - Fk_t[:, :, None, :]
        mask = k_pos[None, :] <= pb[:, None]
        s = jnp.where(mask[None, None], s, -jnp.inf)
        p = jax.nn.softmax(s, axis=-1)
        o = jnp.einsum('bhqk,bkhd->bqhd', p.astype(v.dtype), v, preferred_element_type=jnp.float32)
        return o.astype(q.dtype)

    if T > Q_BLOCK and T % Q_BLOCK == 0:
        nb = T // Q_BLOCK
        qb = jnp.moveaxis(q.reshape(B, nb, Q_BLOCK, H, Dh), 1, 0)
        Fqb = jnp.moveaxis(Fq.reshape(B, nb, Q_BLOCK, H), 1, 0)
        pb = q_pos.reshape(nb, Q_BLOCK)
        o = lax.map(block, (qb, Fqb, pb))
        return jnp.moveaxis(o, 0, 1).reshape(B, T, H, Dh)
    return block((q, Fq, q_pos))


def trunk(x, pos0, pool_prev, past_k, past_v, past_logf, norm_a, w_in_a, w_grp_a, scale_a, w_out_a,
          norm_kv, w_kv, b_f, norm_b, w_in_b, w_out_b, norm_f):
    B, T, _ = x.shape
    new_pool = []
    for l in range(N_A):
        h = rmsnorm(x, norm_a[l])
        z = h @ w_in_a[l]
        u, gate = z[..., :E_A], z[..., E_A:]
        y, st = pool_mix(u, pool_prev[l], pos0, w_grp_a[l], scale_a[l])
        x = x + (y * jax.nn.silu(gate)) @ w_out_a[l]
        new_pool.append(st)
    hk = rmsnorm(x, norm_kv)
    kvf = hk @ w_kv
    k = kvf[..., :W_B].reshape(B, T, N_HEADS, HEAD_DIM)
    v = kvf[..., W_B:2 * W_B].reshape(B, T, N_HEADS, HEAD_DIM)
    logf = jax.nn.log_sigmoid(kvf[..., 2 * W_B:].astype(jnp.float32) + b_f.astype(jnp.float32))
    if past_k is None:
        k_all, v_all, logf_all = k, v, logf
    else:
        k_all = jnp.concatenate([past_k.astype(k.dtype), k], axis=1)
        v_all = jnp.concatenate([past_v.astype(v.dtype), v], axis=1)
        logf_all = jnp.concatenate([past_logf.astype(jnp.float32), logf], axis=1)
    F_all = jnp.cumsum(logf_all, axis=1)
    P = k_all.shape[1] - T
    q_pos = pos0 + jnp.arange(T)
    k_pos = jnp.arange(P + T)
    Fq = F_all[:, P:]
    for l in range(N_B):
        h = rmsnorm(x, norm_b[l])
        z = h @ w_in_b[l]
        q = z[..., :W_B].reshape(B, T, N_HEADS, HEAD_DIM)
        gate = z[..., W_B:]
        o = fox_attention(q, k_all, v_all, Fq, F_all, q_pos, k_pos).reshape(B, T, W_B)
        x = x + (o * jax.nn.silu(gate)) @ w_out_b[l]
    return rmsnorm(x, norm_f), k, v, logf, jnp.stack(new_pool)


def setup_inputs(seed: int = 0) -> dict:
    key = jax.random.key(seed)
    ks = jax.random.split(key, 20)
    f32 = jnp.float32
    nrm = lambda k, s: jax.random.normal(k, s, f32)
    return {
        'x_prompt': nrm(ks[0], (BATCH, SEQ, D_MODEL)),
        'x_sample': nrm(ks[1], (DEC_BATCH, DEC_SEQ, D_MODEL)),
        'cache_k': nrm(ks[2], (DEC_BATCH, PAST_LEN, N_HEADS, HEAD_DIM)),
        'cache_v': nrm(ks[3], (DEC_BATCH, PAST_LEN, N_HEADS, HEAD_DIM)),
        'cache_logf': jax.nn.log_sigmoid(F_BIAS_MEAN + nrm(ks[4], (DEC_BATCH, PAST_LEN, N_HEADS))),
        'state_pool': nrm(ks[5], (N_A, DEC_BATCH, POOL_PAD, E_A)),
        'norm_a': 1.0 + 0.02 * nrm(ks[6], (N_A, D_MODEL)),
        'w_in_a': nrm(ks[7], (N_A, D_MODEL, 2 * E_A)) * D_MODEL ** -0.5,
        'w_grp_a': nrm(ks[8], (N_A, N_POOL_GROUPS, G_A, G_A)) * G_A ** -0.5,
        'scale_a': 1.0 + 0.02 * nrm(ks[9], (N_A, E_A)),
        'w_out_a': nrm(ks[10], (N_A, E_A, D_MODEL)) * E_A ** -0.5,
        'norm_kv': 1.0 + 0.02 * nrm(ks[11], (D_MODEL,)),
        'w_kv': nrm(ks[12], (D_MODEL, 2 * W_B + N_HEADS)) * D_MODEL ** -0.5,
        'b_f': F_BIAS_MEAN + 0.1 * nrm(ks[13], (N_HEADS,)),
        'norm_b': 1.0 + 0.02 * nrm(ks[14], (N_B, D_MODEL)),
        'w_in_b': nrm(ks[15], (N_B, D_MODEL, 2 * W_B)) * D_MODEL ** -0.5,
        'w_out_b': nrm(ks[16], (N_B, W_B, D_MODEL)) * W_B ** -0.5,
        'norm_f': 1.0 + 0.02 * nrm(ks[17], (D_MODEL,)),
    }


def reference(x_prompt, x_sample, cache_k, cache_v, cache_logf, state_pool, norm_a, w_in_a, w_grp_a,
              scale_a, w_out_a, norm_kv, w_kv, b_f, norm_b, w_in_b, w_out_b, norm_f):
    pool_zero = jnp.zeros((N_A, x_prompt.shape[0], POOL_PAD, E_A), x_prompt.dtype)
    y_prompt, k_prompt, v_prompt, logf_prompt, pool_prompt = trunk(
        x_prompt, 0, pool_zero, None, None, None, norm_a, w_in_a, w_grp_a, scale_a, w_out_a,
        norm_kv, w_kv, b_f, norm_b, w_in_b, w_out_b, norm_f)
    y_sample, k_sample, v_sample, logf_sample, pool_sample = trunk(
        x_sample, cache_k.shape[1], state_pool, cache_k, cache_v, cache_logf, norm_a, w_in_a, w_grp_a,
        scale_a, w_out_a, norm_kv, w_kv, b_f, norm_b, w_in_b, w_out_b, norm_f)
    return (y_prompt, y_sample, k_prompt, v_prompt, logf_prompt, pool_prompt,
            k_sample, v_sample, logf_sample, pool_sample)
```

```python
import numpy as np
from contextlib import ExitStack
import concourse.bass as bass
import concourse.mybir as mybir
from concourse.bass_utils import run_bass_kernel_spmd

F32 = mybir.dt.float32
BF16 = mybir.dt.bfloat16
AF = mybir.ActivationFunctionType
ALU = mybir.AluOpType

D = 2048
EA = 4096
H = 16
DH = 128
NT = 512
EPS = 1e-6
SCALE = DH ** -0.5
PAST = 2048
NCORES = 8

C_ID, C_TRI, C_ONE, C_MSK, C_INV, C_G, C_SC, C_BF, C_PM, C_END = 0, 128, 256, 384, 512, 576, 672, 736, 752, 753


class _Op:
    __slots__ = ("eng", "fn", "dma", "slot", "deps", "signal", "event")

    def __init__(self, eng, fn, dma, slot):
        self.eng = eng
        self.fn = fn
        self.dma = dma
        self.slot = slot
        self.deps = []
        self.signal = False
        self.event = None


class Sched:
    ENGS = ("pe", "act", "dve", "pool", "sp")

    def __init__(self):
        self.ops = []
        self.last_writer = {}
        self.readers = {}

    def op(self, eng, fn, reads=(), writes=(), dma=False, slot=None):
        o = _Op(eng, fn, dma, slot)
        deps = []
        for k in reads:
            w = self.last_writer.get(k)
            if w is not None:
                deps.append(w)
            if isinstance(k, tuple) and k[0] == "ps":
                for r in self.readers.get(k, ()):
                    if r.eng != eng:
                        deps.append(r)
        for k in writes:
            w = self.last_writer.get(k)
            if w is not None:
                deps.append(w)
            deps.extend(self.readers.get(k, ()))
        seen = set()
        for d in deps:
            if id(d) in seen:
                continue
            seen.add(id(d))
            if (not d.dma) and d.eng == eng and eng == "pe":
                continue
            o.deps.append(d)
            d.signal = True
        for k in reads:
            lst = self.readers.setdefault(k, [])
            if not dma:
                for i in range(len(lst)):
                    if (not lst[i].dma) and lst[i].eng == eng:
                        lst[i] = o
                        break
                else:
                    lst.append(o)
            else:
                lst.append(o)
        for k in writes:
            self.last_writer[k] = o
            self.readers[k] = []
        if dma:
            o.signal = True
        self.ops.append(o)
        return o

    def emit(self, nc):
        eng_count = {e: 0 for e in self.ENGS}
        slot_count = {}
        for o in self.ops:
            if not o.signal:
                continue
            if o.dma:
                c = slot_count.get(o.slot, 0) + 16
                slot_count[o.slot] = c
                o.event = (("dma", o.slot), c)
            else:
                eng_count[o.eng] += 1
                o.event = (("eng", o.eng), eng_count[o.eng])
        sem_names = [("eng", e) for e in self.ENGS] + [("dma", s) for s in slot_count]
        per_eng = {e: [] for e in self.ENGS}
        for o in self.ops:
            per_eng[o.eng].append(o)
        final_events = {("dma", s): v for s, v in slot_count.items()}
        with ExitStack() as st:
            sems = {}
            for i, k in enumerate(sem_names):
                sems[k] = st.enter_context(nc.semaphore("s%d" % i))
            block = st.enter_context(nc.Block())

            def run(engname, e):
                waited = {}
                for o in per_eng[engname]:
                    for d in o.deps:
                        sk, v = d.event
                        if waited.get(sk, 0) >= v:
                            continue
                        waited[sk] = v
                        e.wait_ge(sems[sk], v)
                    ins = o.fn(e)
                    if o.signal:
                        ins.then_inc(sems[o.event[0]], 16 if o.dma else 1)
                if engname == "sp":
                    for sk, v in final_events.items():
                        if waited.get(sk, 0) < v:
                            e.wait_ge(sems[sk], v)

            @block.tensor
            def _(e):
                run("pe", e)

            @block.scalar
            def _(e):
                run("act", e)

            @block.vector
            def _(e):
                run("dve", e)

            @block.gpsimd
            def _(e):
                run("pool", e)

            @block.sync
            def _(e):
                run("sp", e)
        return eng_count, len(sem_names)


def build_nc(npass=4, with_sample=True):
    SEQP = npass * NT
    NTL = SEQP // 128
    nc = bass.Bass("TRN2", target_bir_lowering=False)

    def din(name, shape, dt=F32):
        return nc.dram_tensor(name, list(shape), dt, kind="ExternalInput").ap()

    def dout(name, shape, dt=F32):
        return nc.dram_tensor(name, list(shape), dt, kind="ExternalOutput").ap()

    xp = din("xp", [SEQP, D])
    xs = din("xs", [32, D])
    xh = din("xh", [32, D])
    ck = din("ck", [2, PAST, D])
    cv = din("cv", [2, PAST, D])
    clf = din("clf", [2, PAST, H])
    stp = din("stp", [2, 2, 15, EA])
    w_in_a = din("w_in_a", [2, D, 2 * EA])
    w_grp_a = din("w_grp_a", [2, 4, 1024, 1024])
    w_out_a = din("w_out_a", [2, EA, D])
    w_kv = din("w_kv", [D, 2 * D + H])
    w_in_b = din("w_in_b", [2, D, 2 * D])
    w_out_b = din("w_out_b", [2, D, D])
    cst = din("cst", [128, C_END])

    yp = dout("yp", [SEQP, D])
    kp = dout("kp", [SEQP, D])
    vp = dout("vp", [SEQP, D])
    lfp = dout("lfp", [SEQP, H])
    poolp = dout("poolp", [2, 15, EA])
    ys = dout("ys", [32, D])
    ks = dout("ks", [32, D])
    vs = dout("vs", [32, D])
    lfs = dout("lfs", [32, H])
    pools = dout("pools", [2, 2, 15, EA])

    KTs = nc.dram_tensor("KTs", [H, 128, SEQP], BF16, kind="Internal").ap()
    Vs = nc.dram_tensor("Vs", [H, 128, SEQP // 128, 128], BF16, kind="Internal").ap()
    LFs = nc.dram_tensor("LFs", [128, NTL * 16], F32, kind="Internal").ap()
    KTg = nc.dram_tensor("KTg", [2 * H * 128, SEQP], BF16, kind="Internal").ap()
    Vg = nc.dram_tensor("Vg", [2 * H * 128, SEQP], BF16, kind="Internal").ap()
    LFg = nc.dram_tensor("LFg", [256, NTL * 16], F32, kind="Internal").ap()
    XS = nc.dram_tensor("XS", [npass, 128, 16 * NT], F32, kind="Internal").ap()
    XSs = nc.dram_tensor("XSs", [128, 16 * 32], F32, kind="Internal").ap()
    CKb = nc.dram_tensor("CKb", [2, PAST, D], BF16, kind="Internal").ap()
    CVb = nc.dram_tensor("CVb", [2, PAST, D], BF16, kind="Internal").ap()
    PAIRS = [[0, 1], [2, 3], [4, 5], [6, 7]]

    S = Sched()
    with ExitStack() as st:
        def sb(name, shape, dt):
            return st.enter_context(nc.sbuf_tensor(name, list(shape), dt))

        xT = sb("xT", [128, 16, NT], F32)
        hT = sb("hT", [128, 16, NT], BF16)
        mT = sb("mT", [128, 32, NT], BF16)
        sgT = sb("sgT", [128, 16, NT], BF16)
        U = [sb("U%d" % i, [128, 16 + NT], F32) for i in range(2)]
        T = [sb("T%d" % i, [128, 16 + NT], F32) for i in range(2)]
        tfix = sb("tfix", [128, 16], F32)
        hist = sb("hist", [128, 2, 32, 16], F32)
        NW = 2
        wbuf = [sb("wbuf%d" % i, [128, 8192], BF16) for i in range(NW)]
        wf = sb("wf", [128, 16, 16], BF16)
        NSTG = 2
        stg = [sb("stg%d" % i, [128, D], F32) for i in range(NSTG)]
        NKV = 3
        kvr = sb("kvr", [128, 6144], BF16)
        kbuf = [kvr[:, i * 1024:(i + 1) * 1024] for i in range(NKV)]
        vbuf = [kvr[:, 3072 + i * 1024: 3072 + (i + 1) * 1024].rearrange("p (j d) -> p j d", j=8) for i in range(NKV)]
        kc = kvr[:, 0:2048]
        KC_KEYS = [("kvk", 0), ("kvk", 1)]
        vc = [kvr[:, 2048:4096], sb("vc1", [128, 2048], BF16)[:]]
        VC_KEYS = [[("kvk", 2), ("kvv", 0)], ["vc1"]]
        KTc = kvr[:, 4096:6144].rearrange("p (h k) -> p h k", h=16)
        KTC_KEYS = [("kvv", 1), ("kvv", 2)]
        qs = sb("qs", [128, 16, 32], BF16)
        pTs = sb("pTs", [128, 16, 16], BF16)
        zerob = sb("zerob", [128, 256], BF16)
        bnew = sb("bnew", [16, 2, 16], F32)
        NPT = 4
        pT = [sb("pT%d" % i, [128, NT], BF16) for i in range(NPT)]
        kf = [sb("kf%d" % i, [128, NT], F32) for i in range(2)]
        kst = [sb("kst%d" % i, [128, 4, 128], F32) for i in range(2)]
        vst = [sb("vst%d" % i, [128, NT], F32) for i in range(2)]
        cs = sb("cs", [128, C_END], F32)
        identb = sb("identb", [128, 128], BF16)
        onesb = sb("onesb", [128, 128], BF16)
        maskb = sb("maskb", [128, 128], BF16)
        epsT = sb("epsT", [128, 1], F32)
        rt = sb("rt", [128, NT], F32)
        sraw2 = sb("sraw2", [128, 256], F32)
        pTs2 = sb("pTs2", [128, 16, 16], BF16)
        rs = sb("rs", [128, NT], F32)
        otmp = sb("otmp", [128, NT], F32)
        lz = sb("lz", [128, 16], F32)
        lsg = sb("lsg", [128, 16], F32)
        lf = sb("lf", [128, 4, 16], F32)
        Ft = sb("Ft", [128, 32, 16], F32)
        acc = [sb("acc%d" % i, [128, 16], F32) for i in range(2)]
        gref = sb("gref", [128, 16], F32)
        grefs = sb("grefs", [128, 4, 16], F32)
        biasT = sb("biasT", [128, 32, 16], F32)
        ps = [st.enter_context(nc.psum_tensor("ps%d" % i, [128, NT], F32)) for i in range(7)]
        psb = st.enter_context(nc.psum_tensor("psb", [128, 1024], BF16))

        ident = cs[:, C_ID:C_ID + 128]
        tri = cs[:, C_TRI:C_TRI + 128]
        onesf = cs[:, C_ONE:C_ONE + 128]

        rot = {}

        def nxt(name, n, base=0):
            v = rot.get(name, 0)
            rot[name] = (v + 1) % n
            return base + v

        def accb():
            return nxt("acc", 4, 0)

        def PS(b):
            return ("ps", b)

        def dma(eng, out, in_, reads, writes, slot):
            return S.op(eng, lambda e: e.dma_start(out=out, in_=in_), reads=reads, writes=writes, dma=True, slot=slot)

        def mm(out, lhsT, rhs, start, stop, reads, writes):
            return S.op("pe", lambda e: e.matmul(out, lhsT=lhsT, rhs=rhs, start=start, stop=stop),
                        reads=reads, writes=writes)

        def tp(out, in_, idn, reads, writes):
            return S.op("pe", lambda e: e.transpose(out, in_, idn), reads=reads, writes=writes)

        def act(out, in_, func, reads, writes, bias=None, scale=1.0):
            if bias is None:
                return S.op("act", lambda e: e.activation(out=out, in_=in_, func=func, scale=scale),
                            reads=reads, writes=writes)
            return S.op("act", lambda e: e.activation(out=out, in_=in_, func=func, bias=bias, scale=scale),
                        reads=reads, writes=writes)

        def tt(out, in0, in1, op, reads, writes, eng="dve"):
            return S.op(eng, lambda e: e.tensor_tensor(out=out, in0=in0, in1=in1, op=op), reads=reads, writes=writes)

        def stt(out, in0, scalar, in1, op0, op1, reads, writes):
            return S.op("dve", lambda e: e.scalar_tensor_tensor(out=out, in0=in0, scalar=scalar, in1=in1,
                                                                 op0=op0, op1=op1), reads=reads, writes=writes)

        def cp(eng, out, in_, reads, writes):
            return S.op(eng, lambda e: e.tensor_copy(out=out, in_=in_), reads=reads, writes=writes)

        plan = []

        created = set()

        conv = []

        def padd(src, KC, CW, tid):
            if plan_mode[0] == "conv":
                conv.append((src, KC, CW, tid))
                created.add(tid)
                return
            mode = "cached" if tid in created else "create"
            created.add(tid)
            plan.append((src, KC, CW, tid, mode))

        plan_mode = ["plan"]

        def plan_layers(which):
            tid = 0
            for l in range(2):
                for g in range(4):
                    for hf in range(2):
                        if "A" in which:
                            padd(w_in_a[l, :, g * 1024 + hf * 512: g * 1024 + hf * 512 + 512], 16, 512, tid)
                        tid += 1
                    for hf in range(2):
                        if "A" in which:
                            padd(w_in_a[l, :, EA + g * 1024 + hf * 512: EA + g * 1024 + hf * 512 + 512], 16, 512, tid)
                        tid += 1
                    if "A" in which:
                        padd(w_grp_a[l, g], 8, 1024, tid)
                    tid += 1
                for t in range(8):
                    if "A" in which:
                        padd(w_out_a[l, :, t * 256:(t + 1) * 256], 32, 256, tid)
                    tid += 1
            for t in range(8):
                if "KV" in which:
                    padd(w_kv[:, t * 512:(t + 1) * 512], 16, 512, tid)
                tid += 1
            for l in range(2):
                for t in range(8):
                    if "B" in which:
                        padd(w_in_b[l, :, t * 512:(t + 1) * 512], 16, 512, tid)
                    tid += 1
                for t in range(4):
                    if "B" in which:
                        padd(w_out_b[l, :, t * 512:(t + 1) * 512], 16, 512, tid)
                    tid += 1
            return tid

        NTILE_PASS = plan_layers(("A",))
        NTILE_PASS_A = len(plan)
        for q in range(npass):
            plan_layers(("A", "KV"))
        plan_mode[0] = "conv"
        plan_layers(("B",))
        plan_mode[0] = "plan"
        wconv = list(conv)
        del conv[:]
        pieces = []
        for bi_ in range(2):
            for pc in range(4):
                r0 = pc * 512
                pieces.append(("cache", CKb[bi_, r0:r0 + 512, :], ck[bi_, r0:r0 + 512, :], ("CKb", bi_, pc)))
                pieces.append(("cache", CVb[bi_, r0:r0 + 512, :], cv[bi_, r0:r0 + 512, :], ("CVb", bi_, pc)))
        while wconv or pieces:
            if wconv:
                conv.append(wconv.pop(0))
            if pieces:
                conv.append(pieces.pop(0))
        A_PHASE_END = len(plan)
        plan_layers(("KV", "B"))
        for q in range(npass):
            plan_layers(("B",))
        wstate = {"issued": 0, "used": 0}
        WC = nc.dram_tensor("WC", [NTILE_PASS, 128, 8192], BF16, kind="Internal").ap()

        def w_issue():
            i = wstate["issued"]
            if i >= len(plan):
                return
            src, KC, CW, ti, mode = plan[i]
            slot = i % NW
            view = wbuf[slot][:, 0:KC * CW].rearrange("p (k c) -> p k c", k=KC)
            if mode != "cached":
                dma("pool", view, src.rearrange("(k p) c -> p k c", p=128), [], [("w", slot)], "w%d" % slot)
                if mode == "create":
                    dma("sp", WC[ti, :, 0:KC * CW], wbuf[slot][:, 0:KC * CW], [("w", slot)], [("WC", ti)],
                        "wst%d" % slot)
            else:
                dma("pool", wbuf[slot][:, 0:KC * CW], WC[ti, :, 0:KC * CW], [("WC", ti)], [("w", slot)],
                    "w%d" % slot)
            wstate["issued"] = i + 1
            if NTILE_PASS_A <= i < A_PHASE_END and conv and (i - NTILE_PASS_A) % 6 == 5:
                conv_issue()
            if i == A_PHASE_END - 1:
                while conv:
                    conv_issue()

        cstate = {"n": 0}

        def conv_issue():
            ent = conv.pop(0)
            n = cstate["n"]
            cstate["n"] = n + 1
            if ent[0] == "cache":
                _, dst, src, key = ent
                dma("pool", dst.rearrange("(j p) c -> p j c", p=128), src.rearrange("(j p) c -> p j c", p=128),
                    [], [key], "conv%d" % n)
                return
            src, KC, CW, tid = ent
            dma("pool", WC[tid, :, 0:KC * CW].rearrange("p (k c) -> p k c", k=KC),
                src.rearrange("(k p) c -> p k c", p=128), [], [("WC", tid)], "conv%d" % n)

        def w_next(KC, CW):
            i = wstate["used"]
            while wstate["issued"] <= i:
                w_issue()
            src, kc, cw, _ti, _mode = plan[i]
            assert (kc, cw) == (KC, CW), (i, kc, cw, KC, CW)
            slot = i % NW
            wstate["used"] = i + 1
            return wbuf[slot][:, 0:KC * CW].rearrange("p (k c) -> p k c", k=KC), ("w", slot)

        def w_done():
            while wstate["issued"] < min(len(plan), wstate["used"] + NW):
                w_issue()

        dma("sp", cs[:], cst, [], ["const"], "cs")
        cp("dve", identb[:], ident, ["const"], ["constb"])
        cp("dve", onesb[:], onesf, ["const"], ["constb"])
        cp("dve", maskb[:], cs[:, C_MSK:C_MSK + 128], ["const"], ["constb"])
        S.op("dve", lambda e: e.memset(epsT[:], EPS), writes=["constb"])
        S.op("dve", lambda e: e.memset(hist[:].rearrange("p a b c -> p (a b c)"), 0.0),
             writes=[("hist", l, c) for l in range(2) for c in range(32)])
        S.op("dve", lambda e: e.memset(acc[0][:], 0.0), writes=["acc0"])
        dma("pool", wf[:], w_kv[:, 2 * D:2 * D + H].rearrange("(k p) c -> p k c", p=128), [], ["wf"], "wf")
        for _ in range(NW):
            w_issue()

        CONST = ["const", "constb"]

        import os as _os
        KSKIP = _os.environ.get("KSKIP", "")
        KSTOP = _os.environ.get("KSTOP", "")

        def load_x(src, tok0, N, force_si=None, col0=0):
            ntile = (N + 127) // 128
            for j in range(ntile):
                rows = min(128, N - j * 128)
                si = nxt("stg", NSTG) if force_si is None else force_si
                dma("sp", stg[si][0:rows, :], src[tok0 + j * 128: tok0 + j * 128 + rows, :], [], [("stg", si)],
                    "stg%d" % si)
                for cb in range(4):
                    b = nxt("tpb", 2, 5)
                    for ci in range(4):
                        c = cb * 4 + ci
                        tp(ps[b][:, ci * 128: ci * 128 + rows], stg[si][0:rows, c * 128:(c + 1) * 128],
                           ident[0:rows, 0:rows], [("stg", si)] + CONST, [PS(b)])
                    cp("dve", xT[:, cb * 4:(cb + 1) * 4, col0 + j * 128: col0 + j * 128 + rows],
                       ps[b][:].rearrange("p (c t) -> p c t", c=4)[:, :, 0:rows],
                       [PS(b)], [("xT", cb * 4 + ci) for ci in range(4)])

        def norm_stats(N):
            for c in range(16):
                act(hT[:, c, 0:N], xT[:, c, 0:N], AF.Square, [("xT", c)], [("hT", c)])
            b = 4
            for c in range(16):
                mm(ps[b][:, 0:N], onesb[:], hT[:, c, 0:N], c == 0, c == 15, [("hT", c)] + CONST, [PS(b)])
            act(rt[:, 0:N], ps[b][:, 0:N], AF.Sqrt, [PS(b)] + CONST, ["rt"], bias=epsT[:, 0:1], scale=1.0 / D)
            S.op("dve", lambda e: e.reciprocal(out=ps[4][:, 0:N], in_=rt[:, 0:N]), reads=["rt"], writes=[PS(4)])

        def norm(gi, N):
            norm_stats(N)
            for c in range(16):
                stt(hT[:, c, 0:N], xT[:, c, 0:N], cs[:, C_G + gi * 16 + c: C_G + gi * 16 + c + 1], ps[4][:, 0:N],
                    ALU.mult, ALU.mult, [("xT", c), PS(4)] + CONST, [("hT", c)])

        def proj(wt, wkey, ci, KC, rhs_of, rhs_keys, N):
            b = accb()
            for k in range(KC):
                mm(ps[b][:, 0:N], wt[:, k, ci * 128:(ci + 1) * 128], rhs_of(k), k == 0, k == KC - 1,
                   [wkey, rhs_keys(k)], [PS(b)])
            return b

        def proj_kouter(wt, wkey, nci, KC, rhs_of, rhs_keys, N):
            banks = [accb() for _ in range(nci)]
            for k in range(KC):
                for ci in range(nci):
                    mm(ps[banks[ci]][:, 0:N], wt[:, k, ci * 128:(ci + 1) * 128], rhs_of(k), k == 0, k == KC - 1,
                       [wkey, rhs_keys(k)], [PS(banks[ci])])
            return banks

        def pool_chain(l, g, c, us, segs, first):
            w = 2 << g
            L0 = segs[0][0] - 16
            L1 = segs[-1][0] + segs[-1][1]
            Uk = ("U", us)
            tt(T[0][:, L0 + 1:L1], U[us][:, L0 + 1:L1], U[us][:, L0:L1 - 1], ALU.add, [Uk], ["T0"])
            last, lk = T[0], "T0"
            if g >= 1:
                tt(T[1][:, L0 + 3:L1], T[0][:, L0 + 3:L1], T[0][:, L0 + 1:L1 - 2], ALU.add, ["T0"], ["T1"])
                last, lk = T[1], "T1"
            if g >= 2:
                tt(T[0][:, L0 + 7:L1], T[1][:, L0 + 7:L1], T[1][:, L0 + 3:L1 - 4], ALU.add, ["T1"], ["T0"])
                last, lk = T[0], "T0"
            if g >= 3:
                tt(T[1][:, L0 + 15:L1], T[0][:, L0 + 15:L1], T[0][:, L0 + 7:L1 - 8], ALU.add, ["T0"], ["T1"])
                last, lk = T[1], "T1"
            oc = 0
            for (c0, n) in segs:
                stt(sgT[:, c, oc:oc + n], last[:, c0:c0 + n], 1.0 / w, U[us][:, c0:c0 + n], ALU.mult, ALU.subtract,
                    [lk, Uk], [("sg", c)])
                if first:
                    tt(tfix[:, 0:16], last[:, c0:c0 + 16], cs[:, C_INV + g * 16: C_INV + g * 16 + 16], ALU.mult,
                       [lk] + CONST, ["tfix"])
                    tt(sgT[:, c, oc:oc + 16], tfix[:, 0:16], U[us][:, c0:c0 + 16], ALU.subtract, ["tfix", Uk],
                       [("sg", c)])
                oc += n

        def a_layer(l, N, first, sample):
            norm(l, N)
            if sample:
                segs = [(16, 16), (48, 16), (80, 32)]
            else:
                segs = [(16, N)]
            for g in range(4):
                for hf in range(2):
                    wt, wk = w_next(16, 512)
                    pre = None
                    if g == 0 and hf == 0:
                        pre = proj_kouter(wt, wk, 4, 16, lambda k: hT[:, k, 0:N], lambda k: ("hT", k), N)
                    for ci in range(4):
                        c = hf * 4 + ci
                        gc = g * 8 + c
                        b = pre[ci] if pre else proj(wt, wk, ci, 16, lambda k: hT[:, k, 0:N], lambda k: ("hT", k), N)
                        us = nxt("U", 2)
                        Uk = ("U", us)
                        hk = ("hist", l, gc)
                        if sample:
                            for bi in range(2):
                                cp("dve", U[us][:, bi * 32: bi * 32 + 16], hs[:, l, gc, bi, :], [("stg", 1)], [Uk])
                            act(U[us][:, 0:64].rearrange("p (b t) -> p b t", b=2)[:, :, 16:32],
                                ps[b][:, 0:32].rearrange("p (b t) -> p b t", b=2), AF.Copy, [PS(b)], [Uk])
                            S.op("dve", lambda e, us=us: e.memset(U[us][:, 64:80], 0.0), writes=[Uk])
                            act(U[us][:, 80:112], ps[b][:, 32:64], AF.Copy, [PS(b)], [Uk])
                            cp("dve", hist[:, l, gc, 0:16], U[us][:, 96:112], [Uk], [hk])
                        else:
                            cp("dve", U[us][:, 0:16], hist[:, l, gc, 0:16], [hk], [Uk])
                            act(U[us][:, 16:16 + N], ps[b][:, 0:N], AF.Copy, [PS(b)], [Uk])
                            cp("dve", hist[:, l, gc, 0:16], U[us][:, N:N + 16], [Uk], [hk])
                        pool_chain(l, g, c, us, segs, first)
                        if sample:
                            sample_pool_out(l, gc, us)
                    w_done()
                for hf in range(2):
                    wt, wk = w_next(16, 512)
                    for ci in range(4):
                        c = hf * 4 + ci
                        b = proj(wt, wk, ci, 16, lambda k: hT[:, k, 0:N], lambda k: ("hT", k), N)
                        act(sgT[:, 8 + c, 0:N], ps[b][:, 0:N], AF.Silu, [PS(b)], [("sg", 8 + c)])
                    w_done()
                wt, wk = w_next(8, 1024)
                for c in range(8):
                    b = proj(wt, wk, c, 8, lambda k: sgT[:, k, 0:N], lambda k: ("sg", k), N)
                    stt(mT[:, g * 8 + c, 0:N], ps[b][:, 0:N],
                        cs[:, C_SC + l * 32 + g * 8 + c: C_SC + l * 32 + g * 8 + c + 1], sgT[:, 8 + c, 0:N],
                        ALU.mult, ALU.mult, [PS(b), ("sg", 8 + c)] + CONST, [("mT", g * 8 + c)])
                w_done()
            for t in range(8):
                wt, wk = w_next(32, 256)
                for ci in range(2):
                    c = t * 2 + ci
                    b = proj(wt, wk, ci, 32, lambda k: mT[:, k, 0:N], lambda k: ("mT", k), N)
                    tt(xT[:, c, 0:N], ps[b][:, 0:N], xT[:, c, 0:N], ALU.add, [PS(b), ("xT", c)], [("xT", c)])
                w_done()

        hs = stg[1][:].rearrange("p (l c b t) -> p l c b t", l=2, c=32, b=2)

        def sample_hist_load():
            S.op("dve", lambda e: e.memset(stg[1][:], 0.0), writes=[("stg", 1)])
            for l in range(2):
                for bi in range(2):
                    for cb in range(8):
                        vi = nxt("vst", 2)
                        dma("sp", vst[vi][0:15, :], stp[l, bi, :, cb * 512:(cb + 1) * 512], [], [("vst", vi)],
                            "vst%d" % vi)
                        tb = nxt("tpb", 2, 5)
                        for ci in range(4):
                            tp(ps[tb][:, ci * 16: ci * 16 + 15], vst[vi][0:15, ci * 128:(ci + 1) * 128],
                               ident[0:15, 0:15], [("vst", vi)] + CONST, [PS(tb)])
                        cp("dve", hs[:, l, cb * 4:(cb + 1) * 4, bi, 1:16],
                           ps[tb][:, 0:64].rearrange("p (c t) -> p c t", c=4)[:, :, 0:15], [PS(tb)], [("stg", 1)])

        def sample_pool_out(l, gc, us):
            for bi in range(2):
                tp(ps[5 + bi][0:15, (gc % 4) * 128:(gc % 4 + 1) * 128], U[us][:, bi * 32 + 17: bi * 32 + 32], ident,
                   [("U", us)] + CONST, [PS(5 + bi)])
            if gc % 4 == 3:
                for bi in range(2):
                    vi = nxt("vst", 2)
                    cp("dve", vst[vi][0:15, :], ps[5 + bi][0:15, :], [PS(5 + bi)], [("vst", vi)])
                    dma("sp", pools[l, bi, :, (gc // 4) * 512:(gc // 4 + 1) * 512], vst[vi][0:15, :],
                        [("vst", vi)], [], "vst%d" % vi)

        def kv_stage(p, tok0, N, sample, kout, vout, lfout):
            ntile = (N + 127) // 128
            norm(2, N)
            kpend = []
            lgroups = [(0, 16), (16, 16)] if sample else [(j * 128, 128) for j in range(ntile)]

            def logf_part1():
                for gi, (t0, rows) in enumerate(lgroups):
                    b = accb()
                    for k in range(16):
                        mm(ps[b][0:rows, 0:16], hT[:, k, t0:t0 + rows], wf[:, k, :], k == 0, k == 15,
                           ["wf", ("hT", k)], [PS(b)])
                    tt(lz[0:rows, :], ps[b][0:rows, 0:16], cs[0:rows, C_BF:C_BF + 16], ALU.add, [PS(b)] + CONST,
                       ["lz"])
                    act(lsg[0:rows, :], lz[0:rows, :], AF.Sigmoid, ["lz"], ["lsg"])
                    act(lf[0:rows, gi, :], lsg[0:rows, :], AF.Ln, ["lsg"], [("lf", gi)])
                    dma("sp", lfout[tok0 + t0: tok0 + t0 + rows, :], lf[0:rows, gi, :], [("lf", gi)], [],
                        "lf%d" % gi)
                    if not sample and "f32mm" not in KSKIP:
                        jt = tok0 // 128 + gi
                        dma("sp", LFs[:, jt * 16:(jt + 1) * 16], lf[:, gi, :], [("lf", gi)], [("LFs", p)],
                            "lfs%d" % gi)

            for t in range(4):
                wt, wk = w_next(16, 512)
                pre = proj_kouter(wt, wk, 4, 16, lambda k: hT[:, k, 0:N], lambda k: ("hT", k), N) if t == 0 else None
                for ci in range(4):
                    hh = t * 4 + ci
                    b = pre[ci] if pre else proj(wt, wk, ci, 16, lambda k: hT[:, k, 0:N], lambda k: ("hT", k), N)
                    act(mT[:, hh, 0:N], ps[b][:, 0:N], AF.Copy, [PS(b)], [("mT", hh)])
                    ki = nxt("kf", 2)
                    cp("dve", kf[ki][:, 0:N], ps[b][:, 0:N], [PS(b)], [("kf", ki)])
                    while kpend:
                        kpend.pop(0)()

                    def k_out(hh=hh, ki=ki):
                        tb = nxt("tpb", 2, 5)
                        for j in range(ntile):
                            rows = min(128, N - j * 128)
                            tp(ps[tb][0:rows, j * 128:(j + 1) * 128], kf[ki][:, j * 128: j * 128 + rows], ident,
                               [("kf", ki)] + CONST, [PS(tb)])
                        rows = min(128, N)
                        cp("dve", kst[ki][0:rows, 0:ntile, :],
                           ps[tb][0:rows, 0:ntile * 128].rearrange("p (j d) -> p j d", j=ntile), [PS(tb)],
                           [("kst", ki)])
                        dma("sp", kout[tok0:tok0 + N, hh * 128:(hh + 1) * 128].rearrange("(j p) d -> p j d", j=ntile),
                            kst[ki][0:rows, 0:ntile, :], [("kst", ki)], [], "kst%d" % ki)
                    kpend.append(k_out)
                w_done()
                if t == 0:
                    logf_part1()
            while kpend:
                kpend.pop(0)()
            if not sample and "ktw" not in KSKIP:
                dma("sp", KTs[:, :, tok0:tok0 + N].rearrange("h d t -> d h t"), mT[:, 0:16, 0:N],
                    [("mT", i) for i in range(16)], [("KTs", p)], "ktw")
            if KSTOP == "kvK":
                return
            vb = mT[:, 16:32, :].rearrange("p h (j d) -> p h j d", j=4)
            groups = [(0, 16), (16, 16)] if sample else [(j * 128, 128) for j in range(ntile)]
            for t in range(4):
                wt, wk = w_next(16, 512)
                for gi, (t0, rows) in enumerate(groups):
                    b = accb()
                    for k in range(16):
                        mm(ps[b][0:rows, :], hT[:, k, t0:t0 + rows], wt[:, k, :], k == 0, k == 15,
                           [wk, ("hT", k)], [PS(b)])
                    vi = nxt("vst", 2)
                    act(vst[vi][0:rows, :], ps[b][0:rows, :], AF.Copy, [PS(b)], [("vst", vi)])
                    dma("sp", vout[tok0 + t0: tok0 + t0 + rows, t * 512:(t + 1) * 512], vst[vi][0:rows, :],
                        [("vst", vi)], [], "vst%d" % vi)
                    cp("dve", vb[0:rows, t * 4:(t + 1) * 4, gi, :],
                       ps[b][0:rows, :].rearrange("p (h d) -> p h d", h=4), [PS(b)],
                       [("mT", 16 + t * 4 + i) for i in range(4)])
                w_done()
            if not sample and "vsw" not in KSKIP:
                dma("sp", Vs[:, :, tok0 // 128: tok0 // 128 + 4, :].rearrange("h p j d -> p h j d"), vb,
                    [("mT", 16 + i) for i in range(16)], [("Vs", p)], "vsw")
            if KSTOP == "kvV":
                return
            for gi, (t0, rows) in enumerate(groups):
                if not sample and "f32mm" not in KSKIP:
                    jt = tok0 // 128 + gi
                    a0, a1 = jt % 2, (jt + 1) % 2
                    if gi == 2:
                        mm(ps[6][:, 0:16], onesf, acc[a0][:], True, True, ["acc%d" % a0] + CONST, [PS(6)])
                        cp("dve", grefs[:, p, :], ps[6][:, 0:16], [PS(6)], ["grefs"])
                    mm(ps[6][:, 16:32], tri, lf[:, gi, :], True, False, [("lf", gi)] + CONST, [PS(6)])
                    mm(ps[6][:, 16:32], onesf, acc[a0][:], False, True, ["acc%d" % a0] + CONST, [PS(6)])
                    cp("dve", Ft[:, NTL + jt, :], ps[6][:, 16:32], [PS(6)], ["Ft"])
                    tt(acc[a1][:], acc[a0][:], lf[:, gi, :], ALU.add, ["acc%d" % a0, ("lf", gi)], ["acc%d" % a1])

        def attention_prompt(p, N):
            nown = 4 * p + 4
            nt = NTL + nown
            npc = (NTL + 7) // 8
            tt(biasT[:, 0:nt, :], grefs[:, p, :].unsqueeze(1).broadcast_to([128, nt, 16]), Ft[:, 0:nt, :],
               ALU.subtract, ["grefs", "Ft"], ["biasT"])
            S.op("dve", lambda e: e.tensor_scalar(out=biasT[:, 0:NTL, :], in0=biasT[:, 0:NTL, :],
                                                  scalar1=cs[:, C_PM:C_PM + 1], scalar2=None, op0=ALU.add),
                 reads=["biasT"] + CONST, writes=["biasT"])
            nchunk = npc + (nown + 7) // 8

            def chunk_info(kc):
                if kc < npc:
                    return kc * 8, min(8, NTL - kc * 8)
                ko = kc - npc
                return NTL + ko * 8, min(8, nown - ko * 8)

            items = [(hh, kc) for hh in range(16) for kc in range(nchunk)]
            kvs = {"issued": 0}

            def kv_issue():
                i = kvs["issued"]
                if i >= len(items):
                    return
                hh, kc = items[i]
                slot = nxt("kv", NKV)
                j0, ntl = chunk_info(kc)
                if kc < npc:
                    r0 = (hh // 4) * 1024 + (hh % 4) * 128
                    dma("sp", kbuf[slot][:, 0:ntl * 128], KTg[r0:r0 + 128, kc * 1024: kc * 1024 + ntl * 128],
                        [("KTg", hh // 4)], [("kvk", slot)], "kvk%d" % slot)
                    dma("sp", vbuf[slot][:, 0:ntl, :].rearrange("p j d -> p (j d)"),
                        Vg[r0:r0 + 128, kc * 1024: kc * 1024 + ntl * 128],
                        [("Vg", hh // 4)], [("kvv", slot)], "kvv%d" % slot)
                else:
                    ko = kc - npc
                    pk = [("KTs", q) for q in range(ko * 2, min(p, ko * 2 + 1) + 1)]
                    pv = [("Vs", q) for q in range(ko * 2, min(p, ko * 2 + 1) + 1)]
                    dma("sp", kbuf[slot][:, 0:ntl * 128], KTs[hh, :, ko * 1024: ko * 1024 + ntl * 128], pk,
                        [("kvk", slot)], "kvk%d" % slot)
                    dma("sp", vbuf[slot][:, 0:ntl, :], Vs[hh, :, ko * 8: ko * 8 + ntl, :], pv, [("kvv", slot)],
                        "kvv%d" % slot)
                items[i] = (hh, kc, slot)
                kvs["issued"] = i + 1

            for _ in range(NKV - 1):
                kv_issue()
            pend = []

            def flush():
                while pend:
                    pend.pop(0)()

            for i in range(len(items)):
                while kvs["issued"] <= i:
                    kv_issue()
                hh, kc, slot = items[i]
                if kc == 0:
                    ob = 3 if hh % 2 == 0 else 5
                    sm = 4 if hh % 2 == 0 else 6
                j0, ntl = chunk_info(kc)
                for jt in range(ntl):
                    j = j0 + jt
                    dj = (j - NTL) - 4 * p
                    c0 = dj * 128 if dj > 0 else 0
                    sbk = nxt("sbk", 3, 0)
                    mm(ps[sbk][:, c0:N], kbuf[slot][:, jt * 128:(jt + 1) * 128], mT[:, hh, c0:N], True, True,
                       [("kvk", slot), ("mT", hh)], [PS(sbk)])
                    pi = nxt("pT", NPT)
                    act(pT[pi][:, c0:N], ps[sbk][:, c0:N], AF.Exp, [PS(sbk), "biasT"], [("pT", pi)],
                        bias=biasT[:, j, hh:hh + 1], scale=SCALE)
                    if dj >= 0:
                        tt(pT[pi][:, c0:c0 + 128], pT[pi][:, c0:c0 + 128], maskb[:], ALU.mult,
                           [("pT", pi)] + CONST, [("pT", pi)])
                    while len(pend) > 1:
                        pend.pop(0)()

                    def pv(hh=hh, kc=kc, slot=slot, jt=jt, j=j, c0=c0, pi=pi, ob=ob, sm=sm, ntl=ntl):
                        mm(ps[ob][:, c0:N], vbuf[slot][:, jt, :], pT[pi][:, c0:N], j == 0, j == nt - 1,
                           [("kvv", slot), ("pT", pi)], [PS(ob)])
                        mm(ps[sm][:, c0:N], onesb[:], pT[pi][:, c0:N], j == 0, j == nt - 1,
                           [("pT", pi)] + CONST, [PS(sm)])
                        if jt == ntl - 1:
                            kv_issue()
                        if j == nt - 1:
                            S.op("dve", lambda e, sm=sm: e.reciprocal(out=rs[:, 0:N], in_=ps[sm][:, 0:N]),
                                 reads=[PS(sm)], writes=["rs"])
                            tt(otmp[:, 0:N], ps[ob][:, 0:N], rs[:, 0:N], ALU.mult, [PS(ob), "rs"], ["otmp"])
                            tt(mT[:, 16 + hh, 0:N], otmp[:, 0:N], sgT[:, hh, 0:N], ALU.mult, ["otmp", ("sg", hh)],
                               [("mT", 16 + hh)])
                    pend.append(pv)
            flush()

        def b_layer(l, p, N, sample):
            norm(3 + l, N)
            for t in range(4):
                wt, wk = w_next(16, 512)
                pre = proj_kouter(wt, wk, 4, 16, lambda k: hT[:, k, 0:N], lambda k: ("hT", k), N) if t == 0 else None
                for ci in range(4):
                    hh = t * 4 + ci
                    b = pre[ci] if pre else proj(wt, wk, ci, 16, lambda k: hT[:, k, 0:N], lambda k: ("hT", k), N)
                    if sample:
                        act(qs[:, hh, 0:N], ps[b][:, 0:N], AF.Copy, [PS(b)], ["qs"])
                    else:
                        act(mT[:, hh, 0:N], ps[b][:, 0:N], AF.Copy, [PS(b)], [("mT", hh)])
                w_done()
            for t in range(4):
                wt, wk = w_next(16, 512)
                for ci in range(4):
                    c = t * 4 + ci
                    b = proj(wt, wk, ci, 16, lambda k: hT[:, k, 0:N], lambda k: ("hT", k), N)
                    act(sgT[:, c, 0:N], ps[b][:, 0:N], AF.Silu, [PS(b)], [("sg", c)])
                w_done()
            if sample:
                attention_sample(l, N)
            else:
                attention_prompt(p, N)
            for t in range(4):
                wt, wk = w_next(16, 512)
                for ci in range(4):
                    c = t * 4 + ci
                    if sample:
                        b = proj(wt, wk, ci, 16, lambda k: sgT[:, k, 0:N], lambda k: ("sg", k), N)
                    else:
                        b = proj(wt, wk, ci, 16, lambda k: mT[:, 16 + k, 0:N], lambda k: ("mT", 16 + k), N)
                    tt(xT[:, c, 0:N], ps[b][:, 0:N], xT[:, c, 0:N], ALU.add, [PS(b), ("xT", c)], [("xT", c)])
                w_done()

        def attention_sample(l, N):
            vb = mT[:, 16:32, :].rearrange("p h (j d) -> p h j d", j=4)
            MTK = [("mT", i) for i in range(32)]
            bias_s = biasT[:].rearrange("p (b j) h -> p b j h", b=2)
            clf_t = otmp[:, 0:256].rearrange("p (j h) -> p j h", j=16)
            fin = kf[0][:, 0:256].rearrange("p (j h) -> p j h", j=16)
            tot = rs[:, 256:512].rearrange("p (j h) -> p j h", j=16)
            Rr = rt[:, 128:400].rearrange("p (j h) -> p j h", j=17)
            sraw = otmp[:, 256:512]
            PSB = ("ps", 7)
            for bi in range(2):
                if l == 0:
                    dma("sp", clf_t, clf[bi].rearrange("(j p) h -> p j h", p=128), [], ["otmp"], "clf")
                    mm(ps[4][:, 0:256], tri, clf_t.rearrange("p j h -> p (j h)"), True, True, ["otmp"] + CONST, [PS(4)])
                    cp("dve", fin.rearrange("p j h -> p (j h)"), ps[4][:, 0:256], [PS(4)], [("kf", 0)])
                    mm(ps[4][:, 0:256], onesf, clf_t.rearrange("p j h -> p (j h)"), True, True, ["otmp"] + CONST, [PS(4)])
                    cp("dve", tot.rearrange("p j h -> p (j h)"), ps[4][:, 0:256], [PS(4)], ["rs"])
                    mm(ps[4][:, 0:16], onesf[0:16, :], lf[0:16, bi, :], True, True, [("lf", bi)] + CONST, [PS(4)])
                    cp("dve", Rr[:, 16, :], ps[4][:, 0:16], [PS(4)], ["rt"])
                    mm(ps[4][0:16, 16:32], tri[0:16, 0:16], lf[0:16, bi, :], True, True, [("lf", bi)] + CONST, [PS(4)])
                    tt(bnew[0:16, bi, :], Rr[0:16, 16, :], ps[4][0:16, 16:32], ALU.subtract, [PS(4), "rt"], ["bnew"])
                    for j in range(15, -1, -1):
                        tt(Rr[:, j, :], Rr[:, j + 1, :], tot[:, j, :], ALU.add, ["rt", "rs"], ["rt"])
                    tt(bias_s[:, bi, :, :], Rr[:, 0:16, :], fin, ALU.subtract, [("kf", 0), "rt"], ["biasT"])
                ob, sm = 3, 4
                S.op("pe", lambda e: e.matmul(ps[ob][:, 0:256], lhsT=onesb[:], rhs=zerob[:], start=True, stop=True,
                                              skip_group_check=True), reads=CONST, writes=[PS(ob)])
                SR = [(sraw, "otmp"), (sraw2[:], "sraw2")]
                PT = [(pTs, "pTs"), (pTs2, "pTs2")]
                spend = []
                for jt in range(17):
                    sbk = nxt("sbk", 3, 0)
                    sr, srk = SR[jt % 2]
                    pt, ptk = PT[jt % 2]
                    if jt < 16:
                        vi = jt % 2
                        dma("sp", kc, CKb[bi, jt * 128:(jt + 1) * 128, :], [("CKb", bi, jt // 4)], KC_KEYS, "kc")
                        dma("sp", vc[vi], CVb[bi, jt * 128:(jt + 1) * 128, :], [("CVb", bi, jt // 4)], VC_KEYS[vi],
                            "vc%d" % vi)
                        for half in range(2):
                            for h8 in range(8):
                                hh = half * 8 + h8
                                S.op("pe", lambda e, hh=hh, h8=h8: e.transpose(psb[:, h8 * 128:(h8 + 1) * 128],
                                                                                 kc[:, hh * 128:(hh + 1) * 128], identb[:]),
                                     reads=KC_KEYS + CONST, writes=[PSB])
                            if half == 0:
                                cp("dve", KTc[:, 0:8, :], psb[:].rearrange("p (h k) -> p h k", h=8), [PSB], KTC_KEYS)
                            else:
                                act(KTc[:, 8:16, :], psb[:].rearrange("p (h k) -> p h k", h=8), AF.Copy, [PSB], KTC_KEYS)
                        for hh in range(16):
                            mm(ps[sbk][:, hh * 16:(hh + 1) * 16], KTc[:, hh, :], qs[:, hh, bi * 16:(bi + 1) * 16], True, True,
                               KTC_KEYS + ["qs"], [PS(sbk)])
                        S.op("dve", lambda e, sbk=sbk, bi=bi, jt=jt, sr=sr: e.scalar_tensor_tensor(
                                out=sr.rearrange("p (h q) -> p h q", h=16),
                                in0=ps[sbk][:, 0:256].rearrange("p (h q) -> p h q", h=16), scalar=SCALE,
                                in1=bias_s[:, bi, jt, :].unsqueeze(2).broadcast_to([128, 16, 16]),
                                op0=ALU.mult, op1=ALU.add), reads=[PS(sbk), "biasT"], writes=[srk])
                        act(pt[:].rearrange("p h q -> p (h q)"), sr, AF.Exp, [srk], [ptk])
                        while spend:
                            spend.pop(0)()

                        def pv(jt=jt, vi=vi, pt=pt, ptk=ptk):
                            for hh in range(16):
                                S.op("pe", lambda e, hh=hh: e.matmul(ps[ob][:, hh * 16:(hh + 1) * 16],
                                                                     lhsT=vc[vi][:, hh * 128:(hh + 1) * 128],
                                                                     rhs=pt[:, hh, :], start=False, stop=True,
                                                                     skip_group_check=True),
                                     reads=VC_KEYS[vi] + [ptk], writes=[PS(ob)])
                            mm(ps[sm][:, 0:256], onesb[:], pt[:].rearrange("p h q -> p (h q)"), jt == 0, False,
                               [ptk] + CONST, [PS(sm)])
                        spend.append(pv)
                    else:
                        for hh in range(16):
                            mm(ps[sbk][0:16, hh * 16:(hh + 1) * 16], mT[:, hh, bi * 16:(bi + 1) * 16],
                               qs[:, hh, bi * 16:(bi + 1) * 16], True, True, MTK + ["qs"], [PS(sbk)])
                        S.op("dve", lambda e, sbk=sbk, bi=bi, sr=sr: e.scalar_tensor_tensor(
                            out=sr[0:16, :].rearrange("p (h q) -> p h q", h=16),
                            in0=ps[sbk][0:16, 0:256].rearrange("p (h q) -> p h q", h=16), scalar=SCALE,
                            in1=bnew[0:16, bi, :].unsqueeze(2).broadcast_to([16, 16, 16]),
                            op0=ALU.mult, op1=ALU.add), reads=[PS(sbk), "bnew"], writes=[srk])
                        act(pt[0:16].rearrange("p h q -> p (h q)"), sr[0:16, :], AF.Exp, [srk], [ptk])
                        S.op("dve", lambda e, pt=pt: e.tensor_tensor(out=pt[0:16], in0=pt[0:16],
                                                                in1=maskb[0:16, 0:16].unsqueeze(1).broadcast_to([16, 16, 16]),
                                                                op=ALU.mult), reads=[ptk] + CONST, writes=[ptk])
                        while spend:
                            spend.pop(0)()
                        for hh in range(16):
                            S.op("pe", lambda e, hh=hh, bi=bi, pt=pt: e.matmul(ps[ob][:, hh * 16:(hh + 1) * 16],
                                                                      lhsT=vb[0:16, hh, bi, :], rhs=pt[0:16, hh, :],
                                                                      start=False, stop=True,
                                                                      skip_group_check=True),
                                 reads=MTK + [ptk], writes=[PS(ob)])
                        S.op("pe", lambda e: e.matmul(ps[ob][:, 0:256], lhsT=onesb[:], rhs=zerob[:], start=False,
                                                      stop=True, skip_group_check=True),
                             reads=CONST, writes=[PS(ob)])
                        mm(ps[sm][:, 0:256], onesb[0:16, :], pt[0:16].rearrange("p h q -> p (h q)"), False, True,
                           [ptk] + CONST, [PS(sm)])
                S.op("dve", lambda e: e.reciprocal(out=rs[:, 0:256], in_=ps[sm][:, 0:256]), reads=[PS(sm)], writes=["rs"])
                tt(otmp[:, 0:256], ps[ob][:, 0:256], rs[:, 0:256], ALU.mult, [PS(ob), "rs"], ["otmp"])
                tt(sgT[:, 0:16, bi * 16:(bi + 1) * 16], otmp[:, 0:256].rearrange("p (h q) -> p h q", h=16),
                   sgT[:, 0:16, bi * 16:(bi + 1) * 16], ALU.mult, ["otmp"] + [("sg", i) for i in range(16)],
                   [("sg", i) for i in range(16)])

        def final_stage(tok0, N, yout):
            ntile = (N + 127) // 128
            norm_stats(N)
            fpend = []
            for c in range(16):
                ki = nxt("kf", 2)
                stt(kf[ki][:, 0:N], xT[:, c, 0:N], cs[:, C_G + 5 * 16 + c: C_G + 5 * 16 + c + 1], ps[4][:, 0:N],
                    ALU.mult, ALU.mult, [("xT", c), PS(4)] + CONST, [("kf", ki)])
                while fpend:
                    fpend.pop(0)()

                def y_out(c=c, ki=ki):
                    tb = nxt("tpb", 2, 5)
                    for j in range(ntile):
                        rows = min(128, N - j * 128)
                        tp(ps[tb][0:rows, j * 128:(j + 1) * 128], kf[ki][:, j * 128: j * 128 + rows], ident,
                           [("kf", ki)] + CONST, [PS(tb)])
                    rows = min(128, N)
                    cp("dve", kst[ki][0:rows, 0:ntile, :],
                       ps[tb][0:rows, 0:ntile * 128].rearrange("p (j d) -> p j d", j=ntile), [PS(tb)], [("kst", ki)])
                    dma("sp", yout[tok0:tok0 + N, c * 128:(c + 1) * 128].rearrange("(j p) d -> p j d", j=ntile),
                        kst[ki][0:rows, 0:ntile, :], [("kst", ki)], [], "kst%d" % ki)
                fpend.append(y_out)
            while fpend:
                fpend.pop(0)()

        def pool_prompt_out():
            for l in range(2):
                for cb in range(8):
                    tb = nxt("tpb", 2, 5)
                    for ci in range(4):
                        gc = cb * 4 + ci
                        tp(ps[tb][0:15, ci * 128:(ci + 1) * 128], hist[:, l, gc, 1:16], ident,
                           [("hist", l, gc)] + CONST, [PS(tb)])
                    vi = nxt("vst", 2)
                    cp("dve", vst[vi][0:15, :], ps[tb][0:15, :], [PS(tb)], [("vst", vi)])
                    dma("sp", poolp[l, :, cb * 512:(cb + 1) * 512], vst[vi][0:15, :], [("vst", vi)], [],
                        "vst%d" % vi)

        S.op("dve", lambda e: e.memset(zerob[:], 0.0), writes=["constb"])
        sample_hist_load()
        load_x(xs, 0, 32, force_si=0)
        load_x(xh, 0, 32, force_si=0, col0=32)
        a_layer(0, 64, False, True)
        a_layer(1, 64, False, True)
        XSK = [("xT", c) for c in range(16)]
        dma("sp", XSs.rearrange("p (c t) -> p c t", c=16), xT[:, :, 0:32], XSK, ["XSs"], "xss")

        XKEYS = [("xT", c) for c in range(16)]
        for p in range(npass):
            tok0 = p * NT
            load_x(xp, tok0, NT)
            a_layer(0, NT, p == 0, False)
            a_layer(1, NT, p == 0, False)
            kv_stage(p, tok0, NT, False, kp, vp, lfp)
            dma("sp", XS[p].rearrange("p (c t) -> p c t", c=16), xT[:], XKEYS, [("XS", p)], "xsp")
        pool_prompt_out()

        KT2 = KTs.rearrange("h d t -> (h d) t")
        V2 = Vs.rearrange("h p j d -> (h p) (j d)")
        allk = [("KTs", q) for q in range(npass)]
        allv = [("Vs", q) for q in range(npass)]
        for i in range(4):
            S.op("pool", lambda e, i=i: e.collective_compute("AllGather", ALU.bypass, replica_groups=PAIRS,
                                                            ins=[KT2[i * 512:(i + 1) * 512, :].opt()],
                                                            outs=[KTg[i * 1024:(i + 1) * 1024, :].opt()]),
                 reads=allk, writes=[("KTg", i)])
            S.op("pool", lambda e, i=i: e.collective_compute("AllGather", ALU.bypass, replica_groups=PAIRS,
                                                            ins=[V2[i * 512:(i + 1) * 512, :].opt()],
                                                            outs=[Vg[i * 1024:(i + 1) * 1024, :].opt()]),
                 reads=allv, writes=[("Vg", i)])
        S.op("pool", lambda e: e.collective_compute("AllGather", ALU.bypass, replica_groups=PAIRS,
                                                    ins=[LFs.opt()], outs=[LFg.opt()]),
             reads=[("LFs", q) for q in range(npass)], writes=["LFg"])

        dma("sp", xT[:, :, 0:32], XSs.rearrange("p (c t) -> p c t", c=16), ["XSs"], XSK, "xsl")
        kv_stage(npass, 0, 32, True, ks, vs, lfs)
        b_layer(0, npass, 32, True)
        b_layer(1, npass, 32, True)
        final_stage(0, 32, ys)

        lfprev = biasT[:, 0:NTL, :]
        W16 = NTL * 16
        tot = rs[:, 256:256 + W16].rearrange("p (j h) -> p j h", j=NTL)
        Rr = rt[:, 128:128 + W16 + 16].rearrange("p (j h) -> p j h", j=NTL + 1)
        dma("sp", lfprev.rearrange("p j h -> p (j h)"), LFg[0:128, :], ["LFg"], ["biasT"], "lfg")
        mm(ps[4][:, 0:W16], onesf, lfprev.rearrange("p j h -> p (j h)"), True, True, ["biasT"] + CONST, [PS(4)])
        cp("dve", tot.rearrange("p j h -> p (j h)"), ps[4][:, 0:W16], [PS(4)], ["rs"])
        S.op("dve", lambda e: e.memset(Rr[:, NTL, :], 0.0), writes=["rt"])
        for j in range(NTL - 1, -1, -1):
            tt(Rr[:, j, :], Rr[:, j + 1, :], tot[:, j, :], ALU.add, ["rt", "rs"], ["rt"])
        mm(ps[4][:, 0:W16], tri, lfprev.rearrange("p j h -> p (j h)"), True, True, ["biasT"] + CONST, [PS(4)])
        tt(Ft[:, 0:NTL, :], ps[4][:, 0:W16].rearrange("p (j h) -> p j h", j=NTL), Rr[:, 0:NTL, :], ALU.subtract,
           [PS(4), "rt"], ["Ft"])

        for p in range(npass):
            tok0 = p * NT
            dma("sp", xT[:], XS[p].rearrange("p (c t) -> p c t", c=16), [("XS", p)], XKEYS, "xld")
            b_layer(0, p, NT, False)
            b_layer(1, p, NT, False)
            final_stage(tok0, NT, yp)

        info = S.emit(nc)
    return nc, info


def _consts(norm_a, norm_kv, norm_b, norm_f, scale_a, b_f):
    c = np.zeros((128, C_END), np.float32)
    c[:, C_ID:C_ID + 128] = np.eye(128, dtype=np.float32)
    i = np.arange(128)
    c[:, C_TRI:C_TRI + 128] = (i[:, None] <= i[None, :]).astype(np.float32)
    c[:, C_ONE:C_ONE + 128] = 1.0
    c[:, C_MSK:C_MSK + 128] = (i[:, None] <= i[None, :]).astype(np.float32)
    for g, w in enumerate((2, 4, 8, 16)):
        c[:, C_INV + g * 16: C_INV + g * 16 + 16] = 1.0 / np.minimum(np.arange(16) + 1, w)
    vecs = np.stack([norm_a[0], norm_a[1], norm_kv, norm_b[0], norm_b[1], norm_f]).astype(np.float32)
    c[:, C_G:C_G + 96] = vecs.reshape(6, 16, 128).transpose(2, 0, 1).reshape(128, 96)
    c[:, C_SC:C_SC + 64] = np.asarray(scale_a, np.float32).reshape(2, 32, 128).transpose(2, 0, 1).reshape(128, 64)
    c[:, C_BF:C_BF + 16] = np.broadcast_to(np.asarray(b_f, np.float32)[None, :], (128, 16))
    return c


def _core_consts(base, half):
    c = base.copy()
    if half == 1:
        for g, w in enumerate((2, 4, 8, 16)):
            c[:, C_INV + g * 16: C_INV + g * 16 + 16] = 1.0 / w
        c[:, C_PM] = 0.0
    else:
        c[:, C_PM] = -30000.0
    return c


def make_in_maps(inputs, npass=4, ncores=NCORES):
    f = lambda a: np.ascontiguousarray(np.asarray(a, dtype=np.float32))
    x_prompt = f(inputs["x_prompt"])
    x_sample = f(inputs["x_sample"])
    cache_k = f(inputs["cache_k"])
    cache_v = f(inputs["cache_v"])
    cache_logf = f(inputs["cache_logf"])
    state_pool = f(inputs["state_pool"])
    cst = _consts(f(inputs["norm_a"]), f(inputs["norm_kv"]), f(inputs["norm_b"]), f(inputs["norm_f"]),
                  f(inputs["scale_a"]), f(inputs["b_f"]))
    shared = {
        "w_in_a": f(inputs["w_in_a"]), "w_grp_a": f(inputs["w_grp_a"]), "w_out_a": f(inputs["w_out_a"]),
        "w_kv": f(inputs["w_kv"]), "w_in_b": f(inputs["w_in_b"]), "w_out_b": f(inputs["w_out_b"]),
    }
    nb = x_prompt.shape[0]
    nsb = x_sample.shape[0]
    HT = npass * NT
    maps = []
    for c in range(ncores):
        b, half = (c // 2) % nb, c % 2
        m = dict(shared)
        m["cst"] = _core_consts(cst, half)
        m["xp"] = np.ascontiguousarray(x_prompt[b, half * HT:(half + 1) * HT])
        m["xh"] = np.ascontiguousarray(x_prompt[b, HT - 32:HT]) if half == 1 else np.zeros((32, D), np.float32)
        sb0 = (2 * c) % nsb
        m["xs"] = np.ascontiguousarray(x_sample[sb0:sb0 + 2].reshape(32, D))
        m["ck"] = np.ascontiguousarray(cache_k[sb0:sb0 + 2].reshape(2, PAST, D))
        m["cv"] = np.ascontiguousarray(cache_v[sb0:sb0 + 2].reshape(2, PAST, D))
        m["clf"] = np.ascontiguousarray(cache_logf[sb0:sb0 + 2])
        m["stp"] = np.ascontiguousarray(state_pool[:, sb0:sb0 + 2])
        maps.append(m)
    return maps


_NC_CACHE = {}


def kernel(**inputs):
    if "nc" not in _NC_CACHE:
        _NC_CACHE["nc"] = build_nc(4, True)[0]
    nc = _NC_CACHE["nc"]
    maps = make_in_maps(inputs, 4, NCORES)
    res = run_bass_kernel_spmd(nc, maps, core_ids=list(range(NCORES)))
    r = res.results
    cat = lambda name, b: np.concatenate([r[2 * b][name], r[2 * b + 1][name]], axis=0)
    y_prompt = np.stack([cat("yp", b) for b in range(4)])
    k_prompt = np.stack([cat("kp", b) for b in range(4)]).reshape(4, 4096, H, DH)
    v_prompt = np.stack([cat("vp", b) for b in range(4)]).reshape(4, 4096, H, DH)
    logf_prompt = np.stack([cat("lfp", b) for b in range(4)])
    pool_prompt = np.stack([r[2 * b + 1]["poolp"] for b in range(4)], axis=1)
    y_sample = np.concatenate([r[c]["ys"].reshape(2, 16, D) for c in range(NCORES)])
    k_sample = np.concatenate([r[c]["ks"].reshape(2, 16, H, DH) for c in range(NCORES)])
    v_sample = np.concatenate([r[c]["vs"].reshape(2, 16, H, DH) for c in range(NCORES)])
    logf_sample = np.concatenate([r[c]["lfs"].reshape(2, 16, H) for c in range(NCORES)])
    pool_sample = np.concatenate([r[c]["pools"] for c in range(NCORES)], axis=1)
    return (y_prompt, y_sample, k_prompt, v_prompt, logf_prompt, pool_prompt,
            k_sample, v_sample, logf_sample, pool_sample)
```

```python
import numpy as np
from contextlib import ExitStack
import concourse.bass as bass
import concourse.mybir as mybir
from concourse.bass_utils import run_bass_kernel_spmd

F32 = mybir.dt.float32
BF16 = mybir.dt.bfloat16
AF = mybir.ActivationFunctionType
ALU = mybir.AluOpType

D = 2048
EA = 4096
H = 16
DH = 128
NT = 512
EPS = 1e-6
SCALE = DH ** -0.5
PAST = 2048
NCORES = 8

C_ID, C_TRI, C_ONE, C_MSK, C_INV, C_G, C_SC, C_BF, C_PM, C_END = 0, 128, 256, 384, 512, 576, 672, 736, 752, 753


class _Op:
    __slots__ = ("eng", "fn", "dma", "slot", "deps", "signal", "event")

    def __init__(self, eng, fn, dma, slot):
        self.eng = eng
        self.fn = fn
        self.dma = dma
        self.slot = slot
        self.deps = []
        self.signal = False
        self.event = None


class Sched:
    ENGS = ("pe", "act", "dve", "pool", "sp")

    def __init__(self):
        self.ops = []
        self.last_writer = {}
        self.readers = {}

    def op(self, eng, fn, reads=(), writes=(), dma=False, slot=None):
        o = _Op(eng, fn, dma, slot)
        deps = []
        for k in reads:
            w = self.last_writer.get(k)
            if w is not None:
                deps.append(w)
            if isinstance(k, tuple) and k[0] == "ps":
                for r in self.readers.get(k, ()):
                    if r.eng != eng:
                        deps.append(r)
        for k in writes:
            w = self.last_writer.get(k)
            if w is not None:
                deps.append(w)
            deps.extend(self.readers.get(k, ()))
        seen = set()
        for d in deps:
            if id(d) in seen:
                continue
            seen.add(id(d))
            if (not d.dma) and d.eng == eng and eng == "pe":
                continue
            o.deps.append(d)
            d.signal = True
        for k in reads:
            lst = self.readers.setdefault(k, [])
            if not dma:
                for i in range(len(lst)):
                    if (not lst[i].dma) and lst[i].eng == eng:
                        lst[i] = o
                        break
                else:
                    lst.append(o)
            else:
                lst.append(o)
        for k in writes:
            self.last_writer[k] = o
            self.readers[k] = []
        if dma:
            o.signal = True
        self.ops.append(o)
        return o

    def emit(self, nc):
        eng_count = {e: 0 for e in self.ENGS}
        slot_count = {}
        for o in self.ops:
            if not o.signal:
                continue
            if o.dma:
                c = slot_count.get(o.slot, 0) + 16
                slot_count[o.slot] = c
                o.event = (("dma", o.slot), c)
            else:
                eng_count[o.eng] += 1
                o.event = (("eng", o.eng), eng_count[o.eng])
        sem_names = [("eng", e) for e in self.ENGS] + [("dma", s) for s in slot_count]
        per_eng = {e: [] for e in self.ENGS}
        for o in self.ops:
            per_eng[o.eng].append(o)
        final_events = {("dma", s): v for s, v in slot_count.items()}
        with ExitStack() as st:
            sems = {}
            for i, k in enumerate(sem_names):
                sems[k] = st.enter_context(nc.semaphore("s%d" % i))
            block = st.enter_context(nc.Block())

            def run(engname, e):
                waited = {}
                for o in per_eng[engname]:
                    for d in o.deps:
                        sk, v = d.event
                        if waited.get(sk, 0) >= v:
                            continue
                        waited[sk] = v
                        e.wait_ge(sems[sk], v)
                    ins = o.fn(e)
                    if o.signal:
                        ins.then_inc(sems[o.event[0]], 16 if o.dma else 1)
                if engname == "sp":
                    for sk, v in final_events.items():
                        if waited.get(sk, 0) < v:
                            e.wait_ge(sems[sk], v)

            @block.tensor
            def _(e):
                run("pe", e)

            @block.scalar
            def _(e):
                run("act", e)

            @block.vector
            def _(e):
                run("dve", e)

            @block.gpsimd
            def _(e):
                run("pool", e)

            @block.sync
            def _(e):
                run("sp", e)
        return eng_count, len(sem_names)


def build_nc(npass=4, with_sample=True):
    SEQP = npass * NT
    NTL = SEQP // 128
    nc = bass.Bass("TRN2", target_bir_lowering=False)

    def din(name, shape, dt=F32):
        return nc.dram_tensor(name, list(shape), dt, kind="ExternalInput").ap()

    def dout(name, shape, dt=F32):
        return nc.dram_tensor(name, list(shape), dt, kind="ExternalOutput").ap()

    xp = din("xp", [SEQP, D])
    xs = din("xs", [32, D])
    xh = din("xh", [32, D])
    ck = din("ck", [2, PAST, D])
    cv = din("cv", [2, PAST, D])
    clf = din("clf", [2, PAST, H])
    stp = din("stp", [2, 2, 15, EA])
    w_in_a = din("w_in_a", [2, D, 2 * EA])
    w_grp_a = din("w_grp_a", [2, 4, 1024, 1024])
    w_out_a = din("w_out_a", [2, EA, D])
    w_kv = din("w_kv", [D, 2 * D + H])
    w_in_b = din("w_in_b", [2, D, 2 * D])
    w_out_b = din("w_out_b", [2, D, D])
    cst = din("cst", [128, C_END])

    yp = dout("yp", [SEQP, D])
    kp = dout("kp", [SEQP, D])
    vp = dout("vp", [SEQP, D])
    lfp = dout("lfp", [SEQP, H])
    poolp = dout("poolp", [2, 15, EA])
    ys = dout("ys", [32, D])
    ks = dout("ks", [32, D])
    vs = dout("vs", [32, D])
    lfs = dout("lfs", [32, H])
    pools = dout("pools", [2, 2, 15, EA])

    KTs = nc.dram_tensor("KTs", [H, 128, SEQP], BF16, kind="Internal").ap()
    Vs = nc.dram_tensor("Vs", [H, 128, SEQP // 128, 128], BF16, kind="Internal").ap()
    LFs = nc.dram_tensor("LFs", [128, NTL * 16], F32, kind="Internal").ap()
    KTg = nc.dram_tensor("KTg", [2 * H * 128, SEQP], BF16, kind="Internal").ap()
    Vg = nc.dram_tensor("Vg", [2 * H * 128, SEQP], BF16, kind="Internal").ap()
    LFg = nc.dram_tensor("LFg", [256, NTL * 16], F32, kind="Internal").ap()
    XS = nc.dram_tensor("XS", [npass, 128, 16 * NT], F32, kind="Internal").ap()
    XSs = nc.dram_tensor("XSs", [128, 16 * 32], F32, kind="Internal").ap()
    CKb = nc.dram_tensor("CKb", [2, PAST, D], BF16, kind="Internal").ap()
    CVb = nc.dram_tensor("CVb", [2, PAST, D], BF16, kind="Internal").ap()
    PAIRS = [[0, 1], [2, 3], [4, 5], [6, 7]]

    S = Sched()
    with ExitStack() as st:
        def sb(name, shape, dt):
            return st.enter_context(nc.sbuf_tensor(name, list(shape), dt))

        xT = sb("xT", [128, 16, NT], F32)
        hT = sb("hT", [128, 16, NT], BF16)
        mT = sb("mT", [128, 32, NT], BF16)
        sgT = sb("sgT", [128, 16, NT], BF16)
        U = [sb("U%d" % i, [128, 16 + NT], F32) for i in range(2)]
        T = [sb("T%d" % i, [128, 16 + NT], F32) for i in range(2)]
        tfix = sb("tfix", [128, 16], F32)
        hist = sb("hist", [128, 2, 32, 16], F32)
        NW = 2
        wbuf = [sb("wbuf%d" % i, [128, 8192], BF16) for i in range(NW)]
        wf = sb("wf", [128, 16, 16], BF16)
        NSTG = 2
        stg = [sb("stg%d" % i, [128, D], F32) for i in range(NSTG)]
        NKV = 3
        kvr = sb("kvr", [128, 6144], BF16)
        kbuf = [kvr[:, i * 1024:(i + 1) * 1024] for i in range(NKV)]
        vbuf = [kvr[:, 3072 + i * 1024: 3072 + (i + 1) * 1024].rearrange("p (j d) -> p j d", j=8) for i in range(NKV)]
        kc = kvr[:, 0:2048]
        KC_KEYS = [("kvk", 0), ("kvk", 1)]
        vc = [kvr[:, 2048:4096], sb("vc1", [128, 2048], BF16)[:]]
        VC_KEYS = [[("kvk", 2), ("kvv", 0)], ["vc1"]]
        KTc = kvr[:, 4096:6144].rearrange("p (h k) -> p h k", h=16)
        KTC_KEYS = [("kvv", 1), ("kvv", 2)]
        qs = sb("qs", [128, 16, 32], BF16)
        pTs = sb("pTs", [128, 16, 16], BF16)
        zerob = sb("zerob", [128, 256], BF16)
        bnew = sb("bnew", [16, 2, 16], F32)
        NPT = 4
        pT = [sb("pT%d" % i, [128, NT], BF16) for i in range(NPT)]
        kf = [sb("kf%d" % i, [128, NT], F32) for i in range(2)]
        kst = [sb("kst%d" % i, [128, 4, 128], F32) for i in range(2)]
        vst = [sb("vst%d" % i, [128, NT], F32) for i in range(2)]
        cs = sb("cs", [128, C_END], F32)
        identb = sb("identb", [128, 128], BF16)
        onesb = sb("onesb", [128, 128], BF16)
        maskb = sb("maskb", [128, 128], BF16)
        epsT = sb("epsT", [128, 1], F32)
        rt = sb("rt", [128, NT], F32)
        sraw2 = sb("sraw2", [128, 256], F32)
        pTs2 = sb("pTs2", [128, 16, 16], BF16)
        rs = sb("rs", [128, NT], F32)
        otmp = sb("otmp", [128, NT], F32)
        lz = sb("lz", [128, 16], F32)
        lsg = sb("lsg", [128, 16], F32)
        lf = sb("lf", [128, 4, 16], F32)
        Ft = sb("Ft", [128, 32, 16], F32)
        acc = [sb("acc%d" % i, [128, 16], F32) for i in range(2)]
        gref = sb("gref", [128, 16], F32)
        grefs = sb("grefs", [128, 4, 16], F32)
        biasT = sb("biasT", [128, 32, 16], F32)
        ps = [st.enter_context(nc.psum_tensor("ps%d" % i, [128, NT], F32)) for i in range(7)]
        psb = st.enter_context(nc.psum_tensor("psb", [128, 1024], BF16))

        ident = cs[:, C_ID:C_ID + 128]
        tri = cs[:, C_TRI:C_TRI + 128]
        onesf = cs[:, C_ONE:C_ONE + 128]

        rot = {}

        def nxt(name, n, base=0):
            v = rot.get(name, 0)
            rot[name] = (v + 1) % n
            return base + v

        def accb():
            return nxt("acc", 4, 0)

        def PS(b):
            return ("ps", b)

        def dma(eng, out, in_, reads, writes, slot):
            return S.op(eng, lambda e: e.dma_start(out=out, in_=in_), reads=reads, writes=writes, dma=True, slot=slot)

        def mm(out, lhsT, rhs, start, stop, reads, writes):
            return S.op("pe", lambda e: e.matmul(out, lhsT=lhsT, rhs=rhs, start=start, stop=stop),
                        reads=reads, writes=writes)

        def tp(out, in_, idn, reads, writes):
            return S.op("pe", lambda e: e.transpose(out, in_, idn), reads=reads, writes=writes)

        def act(out, in_, func, reads, writes, bias=None, scale=1.0):
            if bias is None:
                return S.op("act", lambda e: e.activation(out=out, in_=in_, func=func, scale=scale),
                            reads=reads, writes=writes)
            return S.op("act", lambda e: e.activation(out=out, in_=in_, func=func, bias=bias, scale=scale),
                        reads=reads, writes=writes)

        def tt(out, in0, in1, op, reads, writes, eng="dve"):
            return S.op(eng, lambda e: e.tensor_tensor(out=out, in0=in0, in1=in1, op=op), reads=reads, writes=writes)

        def stt(out, in0, scalar, in1, op0, op1, reads, writes):
            return S.op("dve", lambda e: e.scalar_tensor_tensor(out=out, in0=in0, scalar=scalar, in1=in1,
                                                                 op0=op0, op1=op1), reads=reads, writes=writes)

        def cp(eng, out, in_, reads, writes):
            return S.op(eng, lambda e: e.tensor_copy(out=out, in_=in_), reads=reads, writes=writes)

        plan = []

        created = set()

        conv = []

        def padd(src, KC, CW, tid):
            if plan_mode[0] == "conv":
                conv.append((src, KC, CW, tid))
                created.add(tid)
                return
            mode = "cached" if tid in created else "create"
            created.add(tid)
            plan.append((src, KC, CW, tid, mode))

        plan_mode = ["plan"]

        def plan_layers(which):
            tid = 0
            for l in range(2):
                for g in range(4):
                    for hf in range(2):
                        if "A" in which:
                            padd(w_in_a[l, :, g * 1024 + hf * 512: g * 1024 + hf * 512 + 512], 16, 512, tid)
                        tid += 1
                    for hf in range(2):
                        if "A" in which:
                            padd(w_in_a[l, :, EA + g * 1024 + hf * 512: EA + g * 1024 + hf * 512 + 512], 16, 512, tid)
                        tid += 1
                    if "A" in which:
                        padd(w_grp_a[l, g], 8, 1024, tid)
                    tid += 1
                for t in range(8):
                    if "A" in which:
                        padd(w_out_a[l, :, t * 256:(t + 1) * 256], 32, 256, tid)
                    tid += 1
            for t in range(8):
                if "KV" in which:
                    padd(w_kv[:, t * 512:(t + 1) * 512], 16, 512, tid)
                tid += 1
            for l in range(2):
                for t in range(8):
                    if "B" in which:
                        padd(w_in_b[l, :, t * 512:(t + 1) * 512], 16, 512, tid)
                    tid += 1
                for t in range(4):
                    if "B" in which:
                        padd(w_out_b[l, :, t * 512:(t + 1) * 512], 16, 512, tid)
                    tid += 1
            return tid

        NTILE_PASS = plan_layers(("A",))
        NTILE_PASS_A = len(plan)
        for q in range(npass):
            plan_layers(("A", "KV"))
        plan_mode[0] = "conv"
        plan_layers(("B",))
        plan_mode[0] = "plan"
        wconv = list(conv)
        del conv[:]
        pieces = []
        for bi_ in range(2):
            for pc in range(4):
                r0 = pc * 512
                pieces.append(("cache", CKb[bi_, r0:r0 + 512, :], ck[bi_, r0:r0 + 512, :], ("CKb", bi_, pc)))
                pieces.append(("cache", CVb[bi_, r0:r0 + 512, :], cv[bi_, r0:r0 + 512, :], ("CVb", bi_, pc)))
        while wconv or pieces:
            if wconv:
                conv.append(wconv.pop(0))
            if pieces:
                conv.append(pieces.pop(0))
        A_PHASE_END = len(plan)
        plan_layers(("KV", "B"))
        for q in range(npass):
            plan_layers(("B",))
        wstate = {"issued": 0, "used": 0}
        WC = nc.dram_tensor("WC", [NTILE_PASS, 128, 8192], BF16, kind="Internal").ap()

        def w_issue():
            i = wstate["issued"]
            if i >= len(plan):
                return
            src, KC, CW, ti, mode = plan[i]
            slot = i % NW
            view = wbuf[slot][:, 0:KC * CW].rearrange("p (k c) -> p k c", k=KC)
            if mode != "cached":
                dma("pool", view, src.rearrange("(k p) c -> p k c", p=128), [], [("w", slot)], "w%d" % slot)
                if mode == "create":
                    dma("sp", WC[ti, :, 0:KC * CW], wbuf[slot][:, 0:KC * CW], [("w", slot)], [("WC", ti)],
                        "wst%d" % slot)
            else:
                dma("pool", wbuf[slot][:, 0:KC * CW], WC[ti, :, 0:KC * CW], [("WC", ti)], [("w", slot)],
                    "w%d" % slot)
            wstate["issued"] = i + 1
            if NTILE_PASS_A <= i < A_PHASE_END and conv and (i - NTILE_PASS_A) % 6 == 5:
                conv_issue()
            if i == A_PHASE_END - 1:
                while conv:
                    conv_issue()

        cstate = {"n": 0}

        def conv_issue():
            ent = conv.pop(0)
            n = cstate["n"]
            cstate["n"] = n + 1
            if ent[0] == "cache":
                _, dst, src, key = ent
                dma("pool", dst.rearrange("(j p) c -> p j c", p=128), src.rearrange("(j p) c -> p j c", p=128),
                    [], [key], "conv%d" % n)
                return
            src, KC, CW, tid = ent
            dma("pool", WC[tid, :, 0:KC * CW].rearrange("p (k c) -> p k c", k=KC),
                src.rearrange("(k p) c -> p k c", p=128), [], [("WC", tid)], "conv%d" % n)

        def w_next(KC, CW):
            i = wstate["used"]
            while wstate["issued"] <= i:
                w_issue()
            src, kc, cw, _ti, _mode = plan[i]
            assert (kc, cw) == (KC, CW), (i, kc, cw, KC, CW)
            slot = i % NW
            wstate["used"] = i + 1
            return wbuf[slot][:, 0:KC * CW].rearrange("p (k c) -> p k c", k=KC), ("w", slot)

        def w_done():
            while wstate["issued"] < min(len(plan), wstate["used"] + NW):
                w_issue()

        dma("sp", cs[:], cst, [], ["const"], "cs")
        cp("dve", identb[:], ident, ["const"], ["constb"])
        cp("dve", onesb[:], onesf, ["const"], ["constb"])
        cp("dve", maskb[:], cs[:, C_MSK:C_MSK + 128], ["const"], ["constb"])
        S.op("dve", lambda e: e.memset(epsT[:], EPS), writes=["constb"])
        S.op("dve", lambda e: e.memset(hist[:].rearrange("p a b c -> p (a b c)"), 0.0),
             writes=[("hist", l, c) for l in range(2) for c in range(32)])
        S.op("dve", lambda e: e.memset(acc[0][:], 0.0), writes=["acc0"])
        dma("pool", wf[:], w_kv[:, 2 * D:2 * D + H].rearrange("(k p) c -> p k c", p=128), [], ["wf"], "wf")
        for _ in range(NW):
            w_issue()

        CONST = ["const", "constb"]

        import os as _os
        KSKIP = _os.environ.get("KSKIP", "")
        KSTOP = _os.environ.get("KSTOP", "")

        def load_x(src, tok0, N, force_si=None, col0=0):
            ntile = (N + 127) // 128
            for j in range(ntile):
                rows = min(128, N - j * 128)
                si = nxt("stg", NSTG) if force_si is None else force_si
                dma("sp", stg[si][0:rows, :], src[tok0 + j * 128: tok0 + j * 128 + rows, :], [], [("stg", si)],
                    "stg%d" % si)
                for cb in range(4):
                    b = nxt("tpb", 2, 5)
                    for ci in range(4):
                        c = cb * 4 + ci
                        tp(ps[b][:, ci * 128: ci * 128 + rows], stg[si][0:rows, c * 128:(c + 1) * 128],
                           ident[0:rows, 0:rows], [("stg", si)] + CONST, [PS(b)])
                    cp("dve", xT[:, cb * 4:(cb + 1) * 4, col0 + j * 128: col0 + j * 128 + rows],
                       ps[b][:].rearrange("p (c t) -> p c t", c=4)[:, :, 0:rows],
                       [PS(b)], [("xT", cb * 4 + ci) for ci in range(4)])

        def norm_stats(N):
            for c in range(16):
                act(hT[:, c, 0:N], xT[:, c, 0:N], AF.Square, [("xT", c)], [("hT", c)])
            b = 4
            for c in range(16):
                mm(ps[b][:, 0:N], onesb[:], hT[:, c, 0:N], c == 0, c == 15, [("hT", c)] + CONST, [PS(b)])
            act(rt[:, 0:N], ps[b][:, 0:N], AF.Sqrt, [PS(b)] + CONST, ["rt"], bias=epsT[:, 0:1], scale=1.0 / D)
            S.op("dve", lambda e: e.reciprocal(out=ps[4][:, 0:N], in_=rt[:, 0:N]), reads=["rt"], writes=[PS(4)])

        def norm(gi, N):
            norm_stats(N)
            for c in range(16):
                stt(hT[:, c, 0:N], xT[:, c, 0:N], cs[:, C_G + gi * 16 + c: C_G + gi * 16 + c + 1], ps[4][:, 0:N],
                    ALU.mult, ALU.mult, [("xT", c), PS(4)] + CONST, [("hT", c)])

        def proj(wt, wkey, ci, KC, rhs_of, rhs_keys, N):
            b = accb()
            for k in range(KC):
                mm(ps[b][:, 0:N], wt[:, k, ci * 128:(ci + 1) * 128], rhs_of(k), k == 0, k == KC - 1,
                   [wkey, rhs_keys(k)], [PS(b)])
            return b

        def proj_kouter(wt, wkey, nci, KC, rhs_of, rhs_keys, N):
            banks = [accb() for _ in range(nci)]
            for k in range(KC):
                for ci in range(nci):
                    mm(ps[banks[ci]][:, 0:N], wt[:, k, ci * 128:(ci + 1) * 128], rhs_of(k), k == 0, k == KC - 1,
                       [wkey, rhs_keys(k)], [PS(banks[ci])])
            return banks

        def pool_chain(l, g, c, us, segs, first):
            w = 2 << g
            L0 = segs[0][0] - 16
            L1 = segs[-1][0] + segs[-1][1]
            Uk = ("U", us)
            tt(T[0][:, L0 + 1:L1], U[us][:, L0 + 1:L1], U[us][:, L0:L1 - 1], ALU.add, [Uk], ["T0"])
            last, lk = T[0], "T0"
            if g >= 1:
                tt(T[1][:, L0 + 3:L1], T[0][:, L0 + 3:L1], T[0][:, L0 + 1:L1 - 2], ALU.add, ["T0"], ["T1"])
                last, lk = T[1], "T1"
            if g >= 2:
                tt(T[0][:, L0 + 7:L1], T[1][:, L0 + 7:L1], T[1][:, L0 + 3:L1 - 4], ALU.add, ["T1"], ["T0"])
                last, lk = T[0], "T0"
            if g >= 3:
                tt(T[1][:, L0 + 15:L1], T[0][:, L0 + 15:L1], T[0][:, L0 + 7:L1 - 8], ALU.add, ["T0"], ["T1"])
                last, lk = T[1], "T1"
            oc = 0
            for (c0, n) in segs:
                stt(sgT[:, c, oc:oc + n], last[:, c0:c0 + n], 1.0 / w, U[us][:, c0:c0 + n], ALU.mult, ALU.subtract,
                    [lk, Uk], [("sg", c)])
                if first:
                    tt(tfix[:, 0:16], last[:, c0:c0 + 16], cs[:, C_INV + g * 16: C_INV + g * 16 + 16], ALU.mult,
                       [lk] + CONST, ["tfix"])
                    tt(sgT[:, c, oc:oc + 16], tfix[:, 0:16], U[us][:, c0:c0 + 16], ALU.subtract, ["tfix", Uk],
                       [("sg", c)])
                oc += n

        def a_layer(l, N, first, sample):
            norm(l, N)
            if sample:
                segs = [(16, 16), (48, 16), (80, 32)]
            else:
                segs = [(16, N)]
            for g in range(4):
                for hf in range(2):
                    wt, wk = w_next(16, 512)
                    pre = None
                    if g == 0 and hf == 0:
                        pre = proj_kouter(wt, wk, 4, 16, lambda k: hT[:, k, 0:N], lambda k: ("hT", k), N)
                    for ci in range(4):
                        c = hf * 4 + ci
                        gc = g * 8 + c
                        b = pre[ci] if pre else proj(wt, wk, ci, 16, lambda k: hT[:, k, 0:N], lambda k: ("hT", k), N)
                        us = nxt("U", 2)
                        Uk = ("U", us)
                        hk = ("hist", l, gc)
                        if sample:
                            for bi in range(2):
                                cp("dve", U[us][:, bi * 32: bi * 32 + 16], hs[:, l, gc, bi, :], [("stg", 1)], [Uk])
                            act(U[us][:, 0:64].rearrange("p (b t) -> p b t", b=2)[:, :, 16:32],
                                ps[b][:, 0:32].rearrange("p (b t) -> p b t", b=2), AF.Copy, [PS(b)], [Uk])
                            S.op("dve", lambda e, us=us: e.memset(U[us][:, 64:80], 0.0), writes=[Uk])
                            act(U[us][:, 80:112], ps[b][:, 32:64], AF.Copy, [PS(b)], [Uk])
                            cp("dve", hist[:, l, gc, 0:16], U[us][:, 96:112], [Uk], [hk])
                        else:
                            cp("dve", U[us][:, 0:16], hist[:, l, gc, 0:16], [hk], [Uk])
                            act(U[us][:, 16:16 + N], ps[b][:, 0:N], AF.Copy, [PS(b)], [Uk])
                            cp("dve", hist[:, l, gc, 0:16], U[us][:, N:N + 16], [Uk], [hk])
                        pool_chain(l, g, c, us, segs, first)
                        if sample:
                            sample_pool_out(l, gc, us)
                    w_done()
                for hf in range(2):
                    wt, wk = w_next(16, 512)
                    for ci in range(4):
                        c = hf * 4 + ci
                        b = proj(wt, wk, ci, 16, lambda k: hT[:, k, 0:N], lambda k: ("hT", k), N)
                        act(sgT[:, 8 + c, 0:N], ps[b][:, 0:N], AF.Silu, [PS(b)], [("sg", 8 + c)])
                    w_done()
                wt, wk = w_next(8, 1024)
                for c in range(8):
                    b = proj(wt, wk, c, 8, lambda k: sgT[:, k, 0:N], lambda k: ("sg", k), N)
                    stt(mT[:, g * 8 + c, 0:N], ps[b][:, 0:N],
                        cs[:, C_SC + l * 32 + g * 8 + c: C_SC + l * 32 + g * 8 + c + 1], sgT[:, 8 + c, 0:N],
                        ALU.mult, ALU.mult, [PS(b), ("sg", 8 + c)] + CONST, [("mT", g * 8 + c)])
                w_done()
            for t in range(8):
                wt, wk = w_next(32, 256)
                for ci in range(2):
                    c = t * 2 + ci
                    b = proj(wt, wk, ci, 32, lambda k: mT[:, k, 0:N], lambda k: ("mT", k), N)
                    tt(xT[:, c, 0:N], ps[b][:, 0:N], xT[:, c, 0:N], ALU.add, [PS(b), ("xT", c)], [("xT", c)])
                w_done()

        hs = stg[1][:].rearrange("p (l c b t) -> p l c b t", l=2, c=32, b=2)

        def sample_hist_load():
            S.op("dve", lambda e: e.memset(stg[1][:], 0.0), writes=[("stg", 1)])
            for l in range(2):
                for bi in range(2):
                    for cb in range(8):
                        vi = nxt("vst", 2)
                        dma("sp", vst[vi][0:15, :], stp[l, bi, :, cb * 512:(cb + 1) * 512], [], [("vst", vi)],
                            "vst%d" % vi)
                        tb = nxt("tpb", 2, 5)
                        for ci in range(4):
                            tp(ps[tb][:, ci * 16: ci * 16 + 15], vst[vi][0:15, ci * 128:(ci + 1) * 128],
                               ident[0:15, 0:15], [("vst", vi)] + CONST, [PS(tb)])
                        cp("dve", hs[:, l, cb * 4:(cb + 1) * 4, bi, 1:16],
                           ps[tb][:, 0:64].rearrange("p (c t) -> p c t", c=4)[:, :, 0:15], [PS(tb)], [("stg", 1)])

        def sample_pool_out(l, gc, us):
            for bi in range(2):
                tp(ps[5 + bi][0:15, (gc % 4) * 128:(gc % 4 + 1) * 128], U[us][:, bi * 32 + 17: bi * 32 + 32], ident,
                   [("U", us)] + CONST, [PS(5 + bi)])
            if gc % 4 == 3:
                for bi in range(2):
                    vi = nxt("vst", 2)
                    cp("dve", vst[vi][0:15, :], ps[5 + bi][0:15, :], [PS(5 + bi)], [("vst", vi)])
                    dma("sp", pools[l, bi, :, (gc // 4) * 512:(gc // 4 + 1) * 512], vst[vi][0:15, :],
                        [("vst", vi)], [], "vst%d" % vi)

        def kv_stage(p, tok0, N, sample, kout, vout, lfout):
            ntile = (N + 127) // 128
            norm(2, N)
            kpend = []
            lgroups = [(0, 16), (16, 16)] if sample else [(j * 128, 128) for j in range(ntile)]

            def logf_part1():
                for gi, (t0, rows) in enumerate(lgroups):
                    b = accb()
                    for k in range(16):
                        mm(ps[b][0:rows, 0:16], hT[:, k, t0:t0 + rows], wf[:, k, :], k == 0, k == 15,
                           ["wf", ("hT", k)], [PS(b)])
                    tt(lz[0:rows, :], ps[b][0:rows, 0:16], cs[0:rows, C_BF:C_BF + 16], ALU.add, [PS(b)] + CONST,
                       ["lz"])
                    act(lsg[0:rows, :], lz[0:rows, :], AF.Sigmoid, ["lz"], ["lsg"])
                    act(lf[0:rows, gi, :], lsg[0:rows, :], AF.Ln, ["lsg"], [("lf", gi)])
                    dma("sp", lfout[tok0 + t0: tok0 + t0 + rows, :], lf[0:rows, gi, :], [("lf", gi)], [],
                        "lf%d" % gi)
                    if not sample and "f32mm" not in KSKIP:
                        jt = tok0 // 128 + gi
                        dma("sp", LFs[:, jt * 16:(jt + 1) * 16], lf[:, gi, :], [("lf", gi)], [("LFs", p)],
                            "lfs%d" % gi)

            for t in range(4):
                wt, wk = w_next(16, 512)
                pre = proj_kouter(wt, wk, 4, 16, lambda k: hT[:, k, 0:N], lambda k: ("hT", k), N) if t == 0 else None
                for ci in range(4):
                    hh = t * 4 + ci
                    b = pre[ci] if pre else proj(wt, wk, ci, 16, lambda k: hT[:, k, 0:N], lambda k: ("hT", k), N)
                    act(mT[:, hh, 0:N], ps[b][:, 0:N], AF.Copy, [PS(b)], [("mT", hh)])
                    ki = nxt("kf", 2)
                    cp("dve", kf[ki][:, 0:N], ps[b][:, 0:N], [PS(b)], [("kf", ki)])
                    while kpend:
                        kpend.pop(0)()

                    def k_out(hh=hh, ki=ki):
                        tb = nxt("tpb", 2, 5)
                        for j in range(ntile):
                            rows = min(128, N - j * 128)
                            tp(ps[tb][0:rows, j * 128:(j + 1) * 128], kf[ki][:, j * 128: j * 128 + rows], ident,
                               [("kf", ki)] + CONST, [PS(tb)])
                        rows = min(128, N)
                        cp("dve", kst[ki][0:rows, 0:ntile, :],
                           ps[tb][0:rows, 0:ntile * 128].rearrange("p (j d) -> p j d", j=ntile), [PS(tb)],
                           [("kst", ki)])
                        dma("sp", kout[tok0:tok0 + N, hh * 128:(hh + 1) * 128].rearrange("(j p) d -> p j d", j=ntile),
                            kst[ki][0:rows, 0:ntile, :], [("kst", ki)], [], "kst%d" % ki)
                    kpend.append(k_out)
                w_done()
                if t == 0:
                    logf_part1()
            while kpend:
                kpend.pop(0)()
            if not sample and "ktw" not in KSKIP:
                dma("sp", KTs[:, :, tok0:tok0 + N].rearrange("h d t -> d h t"), mT[:, 0:16, 0:N],
                    [("mT", i) for i in range(16)], [("KTs", p)], "ktw")
            if KSTOP == "kvK":
                return
            vb = mT[:, 16:32, :].rearrange("p h (j d) -> p h j d", j=4)
            groups = [(0, 16), (16, 16)] if sample else [(j * 128, 128) for j in range(ntile)]
            for t in range(4):
                wt, wk = w_next(16, 512)
                for gi, (t0, rows) in enumerate(groups):
                    b = accb()
                    for k in range(16):
                        mm(ps[b][0:rows, :], hT[:, k, t0:t0 + rows], wt[:, k, :], k == 0, k == 15,
                           [wk, ("hT", k)], [PS(b)])
                    vi = nxt("vst", 2)
                    act(vst[vi][0:rows, :], ps[b][0:rows, :], AF.Copy, [PS(b)], [("vst", vi)])
                    dma("sp", vout[tok0 + t0: tok0 + t0 + rows, t * 512:(t + 1) * 512], vst[vi][0:rows, :],
                        [("vst", vi)], [], "vst%d" % vi)
                    cp("dve", vb[0:rows, t * 4:(t + 1) * 4, gi, :],
                       ps[b][0:rows, :].rearrange("p (h d) -> p h d", h=4), [PS(b)],
                       [("mT", 16 + t * 4 + i) for i in range(4)])
                w_done()
            if not sample and "vsw" not in KSKIP:
                dma("sp", Vs[:, :, tok0 // 128: tok0 // 128 + 4, :].rearrange("h p j d -> p h j d"), vb,
                    [("mT", 16 + i) for i in range(16)], [("Vs", p)], "vsw")
            if KSTOP == "kvV":
                return
            for gi, (t0, rows) in enumerate(groups):
                if not sample and "f32mm" not in KSKIP:
                    jt = tok0 // 128 + gi
                    a0, a1 = jt % 2, (jt + 1) % 2
                    if gi == 2:
                        mm(ps[6][:, 0:16], onesf, acc[a0][:], True, True, ["acc%d" % a0] + CONST, [PS(6)])
                        cp("dve", grefs[:, p, :], ps[6][:, 0:16], [PS(6)], ["grefs"])
                    mm(ps[6][:, 16:32], tri, lf[:, gi, :], True, False, [("lf", gi)] + CONST, [PS(6)])
                    mm(ps[6][:, 16:32], onesf, acc[a0][:], False, True, ["acc%d" % a0] + CONST, [PS(6)])
                    cp("dve", Ft[:, NTL + jt, :], ps[6][:, 16:32], [PS(6)], ["Ft"])
                    tt(acc[a1][:], acc[a0][:], lf[:, gi, :], ALU.add, ["acc%d" % a0, ("lf", gi)], ["acc%d" % a1])

        def attention_prompt(p, N):
            nown = 4 * p + 4
            nt = NTL + nown
            npc = (NTL + 7) // 8
            tt(biasT[:, 0:nt, :], grefs[:, p, :].unsqueeze(1).broadcast_to([128, nt, 16]), Ft[:, 0:nt, :],
               ALU.subtract, ["grefs", "Ft"], ["biasT"])
            S.op("dve", lambda e: e.tensor_scalar(out=biasT[:, 0:NTL, :], in0=biasT[:, 0:NTL, :],
                                                  scalar1=cs[:, C_PM:C_PM + 1], scalar2=None, op0=ALU.add),
                 reads=["biasT"] + CONST, writes=["biasT"])
            nchunk = npc + (nown + 7) // 8

            def chunk_info(kc):
                if kc < npc:
                    return kc * 8, min(8, NTL - kc * 8)
                ko = kc - npc
                return NTL + ko * 8, min(8, nown - ko * 8)

            items = [(hh, kc) for hh in range(16) for kc in range(nchunk)]
            kvs = {"issued": 0}

            def kv_issue():
                i = kvs["issued"]
                if i >= len(items):
                    return
                hh, kc = items[i]
                slot = nxt("kv", NKV)
                j0, ntl = chunk_info(kc)
                if kc < npc:
                    r0 = (hh // 4) * 1024 + (hh % 4) * 128
                    dma("sp", kbuf[slot][:, 0:ntl * 128], KTg[r0:r0 + 128, kc * 1024: kc * 1024 + ntl * 128],
                        [("KTg", hh // 4)], [("kvk", slot)], "kvk%d" % slot)
                    dma("sp", vbuf[slot][:, 0:ntl, :].rearrange("p j d -> p (j d)"),
                        Vg[r0:r0 + 128, kc * 1024: kc * 1024 + ntl * 128],
                        [("Vg", hh // 4)], [("kvv", slot)], "kvv%d" % slot)
                else:
                    ko = kc - npc
                    pk = [("KTs", q) for q in range(ko * 2, min(p, ko * 2 + 1) + 1)]
                    pv = [("Vs", q) for q in range(ko * 2, min(p, ko * 2 + 1) + 1)]
                    dma("sp", kbuf[slot][:, 0:ntl * 128], KTs[hh, :, ko * 1024: ko * 1024 + ntl * 128], pk,
                        [("kvk", slot)], "kvk%d" % slot)
                    dma("sp", vbuf[slot][:, 0:ntl, :], Vs[hh, :, ko * 8: ko * 8 + ntl, :], pv, [("kvv", slot)],
                        "kvv%d" % slot)
                items[i] = (hh, kc, slot)
                kvs["issued"] = i + 1

            for _ in range(NKV - 1):
                kv_issue()
            pend = []

            def flush():
                while pend:
                    pend.pop(0)()

            for i in range(len(items)):
                while kvs["issued"] <= i:
                    kv_issue()
                hh, kc, slot = items[i]
                if kc == 0:
                    ob = 3 if hh % 2 == 0 else 5
                    sm = 4 if hh % 2 == 0 else 6
                j0, ntl = chunk_info(kc)
                for jt in range(ntl):
                    j = j0 + jt
                    dj = (j - NTL) - 4 * p
                    c0 = dj * 128 if dj > 0 else 0
                    sbk = nxt("sbk", 3, 0)
                    mm(ps[sbk][:, c0:N], kbuf[slot][:, jt * 128:(jt + 1) * 128], mT[:, hh, c0:N], True, True,
                       [("kvk", slot), ("mT", hh)], [PS(sbk)])
                    pi = nxt("pT", NPT)
                    act(pT[pi][:, c0:N], ps[sbk][:, c0:N], AF.Exp, [PS(sbk), "biasT"], [("pT", pi)],
                        bias=biasT[:, j, hh:hh + 1], scale=SCALE)
                    if dj >= 0:
                        tt(pT[pi][:, c0:c0 + 128], pT[pi][:, c0:c0 + 128], maskb[:], ALU.mult,
                           [("pT", pi)] + CONST, [("pT", pi)])
                    while len(pend) > 1:
                        pend.pop(0)()

                    def pv(hh=hh, kc=kc, slot=slot, jt=jt, j=j, c0=c0, pi=pi, ob=ob, sm=sm, ntl=ntl):
                        mm(ps[ob][:, c0:N], vbuf[slot][:, jt, :], pT[pi][:, c0:N], j == 0, j == nt - 1,
                           [("kvv", slot), ("pT", pi)], [PS(ob)])
                        mm(ps[sm][:, c0:N], onesb[:], pT[pi][:, c0:N], j == 0, j == nt - 1,
                           [("pT", pi)] + CONST, [PS(sm)])
                        if jt == ntl - 1:
                            kv_issue()
                        if j == nt - 1:
                            S.op("dve", lambda e, sm=sm: e.reciprocal(out=rs[:, 0:N], in_=ps[sm][:, 0:N]),
                                 reads=[PS(sm)], writes=["rs"])
                            tt(otmp[:, 0:N], ps[ob][:, 0:N], rs[:, 0:N], ALU.mult, [PS(ob), "rs"], ["otmp"])
                            tt(mT[:, 16 + hh, 0:N], otmp[:, 0:N], sgT[:, hh, 0:N], ALU.mult, ["otmp", ("sg", hh)],
                               [("mT", 16 + hh)])
                    pend.append(pv)
            flush()

        def b_layer(l, p, N, sample):
            norm(3 + l, N)
            for t in range(4):
                wt, wk = w_next(16, 512)
                pre = proj_kouter(wt, wk, 4, 16, lambda k: hT[:, k, 0:N], lambda k: ("hT", k), N) if t == 0 else None
                for ci in range(4):
                    hh = t * 4 + ci
                    b = pre[ci] if pre else proj(wt, wk, ci, 16, lambda k: hT[:, k, 0:N], lambda k: ("hT", k), N)
                    if sample:
                        act(qs[:, hh, 0:N], ps[b][:, 0:N], AF.Copy, [PS(b)], ["qs"])
                    else:
                        act(mT[:, hh, 0:N], ps[b][:, 0:N], AF.Copy, [PS(b)], [("mT", hh)])
                w_done()
            for t in range(4):
                wt, wk = w_next(16, 512)
                for ci in range(4):
                    c = t * 4 + ci
                    b = proj(wt, wk, ci, 16, lambda k: hT[:, k, 0:N], lambda k: ("hT", k), N)
                    act(sgT[:, c, 0:N], ps[b][:, 0:N], AF.Silu, [PS(b)], [("sg", c)])
                w_done()
            if sample:
                attention_sample(l, N)
            else:
                attention_prompt(p, N)
            for t in range(4):
                wt, wk = w_next(16, 512)
                for ci in range(4):
                    c = t * 4 + ci
                    if sample:
                        b = proj(wt, wk, ci, 16, lambda k: sgT[:, k, 0:N], lambda k: ("sg", k), N)
                    else:
                        b = proj(wt, wk, ci, 16, lambda k: mT[:, 16 + k, 0:N], lambda k: ("mT", 16 + k), N)
                    tt(xT[:, c, 0:N], ps[b][:, 0:N], xT[:, c, 0:N], ALU.add, [PS(b), ("xT", c)], [("xT", c)])
                w_done()

        def attention_sample(l, N):
            vb = mT[:, 16:32, :].rearrange("p h (j d) -> p h j d", j=4)
            MTK = [("mT", i) for i in range(32)]
            bias_s = biasT[:].rearrange("p (b j) h -> p b j h", b=2)
            clf_t = otmp[:, 0:256].rearrange("p (j h) -> p j h", j=16)
            fin = kf[0][:, 0:256].rearrange("p (j h) -> p j h", j=16)
            tot = rs[:, 256:512].rearrange("p (j h) -> p j h", j=16)
            Rr = rt[:, 128:400].rearrange("p (j h) -> p j h", j=17)
            sraw = otmp[:, 256:512]
            PSB = ("ps", 7)
            for bi in range(2):
                if l == 0:
                    dma("sp", clf_t, clf[bi].rearrange("(j p) h -> p j h", p=128), [], ["otmp"], "clf")
                    mm(ps[4][:, 0:256], tri, clf_t.rearrange("p j h -> p (j h)"), True, True, ["otmp"] + CONST, [PS(4)])
                    cp("dve", fin.rearrange("p j h -> p (j h)"), ps[4][:, 0:256], [PS(4)], [("kf", 0)])
                    mm(ps[4][:, 0:256], onesf, clf_t.rearrange("p j h -> p (j h)"), True, True, ["otmp"] + CONST, [PS(4)])
                    cp("dve", tot.rearrange("p j h -> p (j h)"), ps[4][:, 0:256], [PS(4)], ["rs"])
                    mm(ps[4][:, 0:16], onesf[0:16, :], lf[0:16, bi, :], True, True, [("lf", bi)] + CONST, [PS(4)])
                    cp("dve", Rr[:, 16, :], ps[4][:, 0:16], [PS(4)], ["rt"])
                    mm(ps[4][0:16, 16:32], tri[0:16, 0:16], lf[0:16, bi, :], True, True, [("lf", bi)] + CONST, [PS(4)])
                    tt(bnew[0:16, bi, :], Rr[0:16, 16, :], ps[4][0:16, 16:32], ALU.subtract, [PS(4), "rt"], ["bnew"])
                    for j in range(15, -1, -1):
                        tt(Rr[:, j, :], Rr[:, j + 1, :], tot[:, j, :], ALU.add, ["rt", "rs"], ["rt"])
                    tt(bias_s[:, bi, :, :], Rr[:, 0:16, :], fin, ALU.subtract, [("kf", 0), "rt"], ["biasT"])
                ob, sm = 3, 4
                S.op("pe", lambda e: e.matmul(ps[ob][:, 0:256], lhsT=onesb[:], rhs=zerob[:], start=True, stop=True,
                                              skip_group_check=True), reads=CONST, writes=[PS(ob)])
                SR = [(sraw, "otmp"), (sraw2[:], "sraw2")]
                PT = [(pTs, "pTs"), (pTs2, "pTs2")]
                spend = []
                for jt in range(17):
                    sbk = nxt("sbk", 3, 0)
                    sr, srk = SR[jt % 2]
                    pt, ptk = PT[jt % 2]
                    if jt < 16:
                        vi = jt % 2
                        dma("sp", kc, CKb[bi, jt * 128:(jt + 1) * 128, :], [("CKb", bi, jt // 4)], KC_KEYS, "kc")
                        dma("sp", vc[vi], CVb[bi, jt * 128:(jt + 1) * 128, :], [("CVb", bi, jt // 4)], VC_KEYS[vi],
                            "vc%d" % vi)
                        for half in range(2):
                            for h8 in range(8):
                                hh = half * 8 + h8
                                S.op("pe", lambda e, hh=hh, h8=h8: e.transpose(psb[:, h8 * 128:(h8 + 1) * 128],
                                                                                 kc[:, hh * 128:(hh + 1) * 128], identb[:]),
                                     reads=KC_KEYS + CONST, writes=[PSB])
                            if half == 0:
                                cp("dve", KTc[:, 0:8, :], psb[:].rearrange("p (h k) -> p h k", h=8), [PSB], KTC_KEYS)
                            else:
                                act(KTc[:, 8:16, :], psb[:].rearrange("p (h k) -> p h k", h=8), AF.Copy, [PSB], KTC_KEYS)
                        for hh in range(16):
                            mm(ps[sbk][:, hh * 16:(hh + 1) * 16], KTc[:, hh, :], qs[:, hh, bi * 16:(bi + 1) * 16], True, True,
                               KTC_KEYS + ["qs"], [PS(sbk)])
                        S.op("dve", lambda e, sbk=sbk, bi=bi, jt=jt, sr=sr: e.scalar_tensor_tensor(
                                out=sr.rearrange("p (h q) -> p h q", h=16),
                                in0=ps[sbk][:, 0:256].rearrange("p (h q) -> p h q", h=16), scalar=SCALE,
                                in1=bias_s[:, bi, jt, :].unsqueeze(2).broadcast_to([128, 16, 16]),
                                op0=ALU.mult, op1=ALU.add), reads=[PS(sbk), "biasT"], writes=[srk])
                        act(pt[:].rearrange("p h q -> p (h q)"), sr, AF.Exp, [srk], [ptk])
                        while spend:
                            spend.pop(0)()

                        def pv(jt=jt, vi=vi, pt=pt, ptk=ptk):
                            for hh in range(16):
                                S.op("pe", lambda e, hh=hh: e.matmul(ps[ob][:, hh * 16:(hh + 1) * 16],
                                                                     lhsT=vc[vi][:, hh * 128:(hh + 1) * 128],
                                                                     rhs=pt[:, hh, :], start=False, stop=True,
                                                                     skip_group_check=True),
                                     reads=VC_KEYS[vi] + [ptk], writes=[PS(ob)])
                            mm(ps[sm][:, 0:256], onesb[:], pt[:].rearrange("p h q -> p (h q)"), jt == 0, False,
                               [ptk] + CONST, [PS(sm)])
                        spend.append(pv)
                    else:
                        for hh in range(16):
                            mm(ps[sbk][0:16, hh * 16:(hh + 1) * 16], mT[:, hh, bi * 16:(bi + 1) * 16],
                               qs[:, hh, bi * 16:(bi + 1) * 16], True, True, MTK + ["qs"], [PS(sbk)])
                        S.op("dve", lambda e, sbk=sbk, bi=bi, sr=sr: e.scalar_tensor_tensor(
                            out=sr[0:16, :].rearrange("p (h q) -> p h q", h=16),
                            in0=ps[sbk][0:16, 0:256].rearrange("p (h q) -> p h q", h=16), scalar=SCALE,
                            in1=bnew[0:16, bi, :].unsqueeze(2).broadcast_to([16, 16, 16]),
                            op0=ALU.mult, op1=ALU.add), reads=[PS(sbk), "bnew"], writes=[srk])
                        act(pt[0:16].rearrange("p h q -> p (h q)"), sr[0:16, :], AF.Exp, [srk], [ptk])
                        S.op("dve", lambda e, pt=pt: e.tensor_tensor(out=pt[0:16], in0=pt[0:16],
                                                                in1=maskb[0:16, 0:16].unsqueeze(1).broadcast_to([16, 16, 16]),
                                                                op=ALU.mult), reads=[ptk] + CONST, writes=[ptk])
                        while spend:
                            spend.pop(0)()
                        for hh in range(16):
                            S.op("pe", lambda e, hh=hh, bi=bi, pt=pt: e.matmul(ps[ob][:, hh * 16:(hh + 1) * 16],
                                                                      lhsT=vb[0:16, hh, bi, :], rhs=pt[0:16, hh, :],
                                                                      start=False, stop=True,
                                                                      skip_group_check=True),
                                 reads=MTK + [ptk], writes=[PS(ob)])
                        S.op("pe", lambda e: e.matmul(ps[ob][:, 0:256], lhsT=onesb[:], rhs=zerob[:], start=False,
                                                      stop=True, skip_group_check=True),
                             reads=CONST, writes=[PS(ob)])
                        mm(ps[sm][:, 0:256], onesb[0:16, :], pt[0:16].rearrange("p h q -> p (h q)"), False, True,
                           [ptk] + CONST, [PS(sm)])
                S.op("dve", lambda e: e.reciprocal(out=rs[:, 0:256], in_=ps[sm][:, 0:256]), reads=[PS(sm)], writes=["rs"])
                tt(otmp[:, 0:256], ps[ob][:, 0:256], rs[:, 0:256], ALU.mult, [PS(ob), "rs"], ["otmp"])
                tt(sgT[:, 0:16, bi * 16:(bi + 1) * 16], otmp[:, 0:256].rearrange("p (h q) -> p h q", h=16),
                   sgT[:, 0:16, bi * 16:(bi + 1) * 16], ALU.mult, ["otmp"] + [("sg", i) for i in range(16)],
                   [("sg", i) for i in range(16)])

        def final_stage(tok0, N, yout, after_chunk=None):
            ntile = (N + 127) // 128
            norm_stats(N)
            fpend = []
            for c in range(16):
                ki = nxt("kf", 2)
                stt(kf[ki][:, 0:N], xT[:, c, 0:N], cs[:, C_G + 5 * 16 + c: C_G + 5 * 16 + c + 1], ps[4][:, 0:N],
                    ALU.mult, ALU.mult, [("xT", c), PS(4)] + CONST, [("kf", ki)])
                while fpend:
                    fpend.pop(0)()

                def y_out(c=c, ki=ki):
                    tb = nxt("tpb", 2, 5)
                    for j in range(ntile):
                        rows = min(128, N - j * 128)
                        tp(ps[tb][0:rows, j * 128:(j + 1) * 128], kf[ki][:, j * 128: j * 128 + rows], ident,
                           [("kf", ki)] + CONST, [PS(tb)])
                    rows = min(128, N)
                    cp("dve", kst[ki][0:rows, 0:ntile, :],
                       ps[tb][0:rows, 0:ntile * 128].rearrange("p (j d) -> p j d", j=ntile), [PS(tb)], [("kst", ki)])
                    dma("sp", yout[tok0:tok0 + N, c * 128:(c + 1) * 128].rearrange("(j p) d -> p j d", j=ntile),
                        kst[ki][0:rows, 0:ntile, :], [("kst", ki)], [], "kst%d" % ki)
                fpend.append(y_out)
                if after_chunk is not None:
                    after_chunk(c)
            while fpend:
                fpend.pop(0)()

        def pool_prompt_out():
            for l in range(2):
                for cb in range(8):
                    tb = nxt("tpb", 2, 5)
                    for ci in range(4):
                        gc = cb * 4 + ci
                        tp(ps[tb][0:15, ci * 128:(ci + 1) * 128], hist[:, l, gc, 1:16], ident,
                           [("hist", l, gc)] + CONST, [PS(tb)])
                    vi = nxt("vst", 2)
                    cp("dve", vst[vi][0:15, :], ps[tb][0:15, :], [PS(tb)], [("vst", vi)])
                    dma("sp", poolp[l, :, cb * 512:(cb + 1) * 512], vst[vi][0:15, :], [("vst", vi)], [],
                        "vst%d" % vi)

        S.op("dve", lambda e: e.memset(zerob[:], 0.0), writes=["constb"])
        sample_hist_load()
        load_x(xs, 0, 32, force_si=0)
        load_x(xh, 0, 32, force_si=0, col0=32)
        a_layer(0, 64, False, True)
        a_layer(1, 64, False, True)
        XSK = [("xT", c) for c in range(16)]
        dma("sp", XSs.rearrange("p (c t) -> p c t", c=16), xT[:, :, 0:32], XSK, ["XSs"], "xss")

        XKEYS = [("xT", c) for c in range(16)]
        for p in range(npass):
            tok0 = p * NT
            load_x(xp, tok0, NT)
            a_layer(0, NT, p == 0, False)
            a_layer(1, NT, p == 0, False)
            kv_stage(p, tok0, NT, False, kp, vp, lfp)
            dma("sp", XS[p].rearrange("p (c t) -> p c t", c=16), xT[:], XKEYS, [("XS", p)], "xsp")
        pool_prompt_out()

        KT2 = KTs.rearrange("h d t -> (h d) t")
        V2 = Vs.rearrange("h p j d -> (h p) (j d)")
        allk = [("KTs", q) for q in range(npass)]
        allv = [("Vs", q) for q in range(npass)]
        for i in range(4):
            S.op("pool", lambda e, i=i: e.collective_compute("AllGather", ALU.bypass, replica_groups=PAIRS,
                                                            ins=[KT2[i * 512:(i + 1) * 512, :].opt()],
                                                            outs=[KTg[i * 1024:(i + 1) * 1024, :].opt()]),
                 reads=allk, writes=[("KTg", i)])
            S.op("pool", lambda e, i=i: e.collective_compute("AllGather", ALU.bypass, replica_groups=PAIRS,
                                                            ins=[V2[i * 512:(i + 1) * 512, :].opt()],
                                                            outs=[Vg[i * 1024:(i + 1) * 1024, :].opt()]),
                 reads=allv, writes=[("Vg", i)])
        S.op("pool", lambda e: e.collective_compute("AllGather", ALU.bypass, replica_groups=PAIRS,
                                                    ins=[LFs.opt()], outs=[LFg.opt()]),
             reads=[("LFs", q) for q in range(npass)], writes=["LFg"])

        dma("sp", xT[:, :, 0:32], XSs.rearrange("p (c t) -> p c t", c=16), ["XSs"], XSK, "xsl")
        kv_stage(npass, 0, 32, True, ks, vs, lfs)
        b_layer(0, npass, 32, True)
        b_layer(1, npass, 32, True)
        final_stage(0, 32, ys)

        lfprev = biasT[:, 0:NTL, :]
        W16 = NTL * 16
        tot = rs[:, 256:256 + W16].rearrange("p (j h) -> p j h", j=NTL)
        Rr = rt[:, 128:128 + W16 + 16].rearrange("p (j h) -> p j h", j=NTL + 1)
        dma("sp", lfprev.rearrange("p j h -> p (j h)"), LFg[0:128, :], ["LFg"], ["biasT"], "lfg")
        mm(ps[4][:, 0:W16], onesf, lfprev.rearrange("p j h -> p (j h)"), True, True, ["biasT"] + CONST, [PS(4)])
        cp("dve", tot.rearrange("p j h -> p (j h)"), ps[4][:, 0:W16], [PS(4)], ["rs"])
        S.op("dve", lambda e: e.memset(Rr[:, NTL, :], 0.0), writes=["rt"])
        for j in range(NTL - 1, -1, -1):
            tt(Rr[:, j, :], Rr[:, j + 1, :], tot[:, j, :], ALU.add, ["rt", "rs"], ["rt"])
        mm(ps[4][:, 0:W16], tri, lfprev.rearrange("p j h -> p (j h)"), True, True, ["biasT"] + CONST, [PS(4)])
        tt(Ft[:, 0:NTL, :], ps[4][:, 0:W16].rearrange("p (j h) -> p j h", j=NTL), Rr[:, 0:NTL, :], ALU.subtract,
           [PS(4), "rt"], ["Ft"])

        def reload_chunk(pn, c):
            if c % 4 == 3:
                c0 = c - 3
                dma("sp", xT[:, c0:c0 + 4, :], XS[pn][:, c0 * NT:(c0 + 4) * NT].rearrange("p (c t) -> p c t", c=4),
                    [("XS", pn)], [("xT", c0 + i) for i in range(4)], "xld%d" % (c // 4))

        dma("sp", xT[:], XS[0].rearrange("p (c t) -> p c t", c=16), [("XS", 0)], XKEYS, "xld")
        for p in range(npass):
            tok0 = p * NT
            b_layer(0, p, NT, False)
            b_layer(1, p, NT, False)
            final_stage(tok0, NT, yp,
                        after_chunk=(lambda c, p=p: reload_chunk(p + 1, c)) if p + 1 < npass else None)

        info = S.emit(nc)
    return nc, info


def _consts(norm_a, norm_kv, norm_b, norm_f, scale_a, b_f):
    c = np.zeros((128, C_END), np.float32)
    c[:, C_ID:C_ID + 128] = np.eye(128, dtype=np.float32)
    i = np.arange(128)
    c[:, C_TRI:C_TRI + 128] = (i[:, None] <= i[None, :]).astype(np.float32)
    c[:, C_ONE:C_ONE + 128] = 1.0
    c[:, C_MSK:C_MSK + 128] = (i[:, None] <= i[None, :]).astype(np.float32)
    for g, w in enumerate((2, 4, 8, 16)):
        c[:, C_INV + g * 16: C_INV + g * 16 + 16] = 1.0 / np.minimum(np.arange(16) + 1, w)
    vecs = np.stack([norm_a[0], norm_a[1], norm_kv, norm_b[0], norm_b[1], norm_f]).astype(np.float32)
    c[:, C_G:C_G + 96] = vecs.reshape(6, 16, 128).transpose(2, 0, 1).reshape(128, 96)
    c[:, C_SC:C_SC + 64] = np.asarray(scale_a, np.float32).reshape(2, 32, 128).transpose(2, 0, 1).reshape(128, 64)
    c[:, C_BF:C_BF + 16] = np.broadcast_to(np.asarray(b_f, np.float32)[None, :], (128, 16))
    return c


def _core_consts(base, half):
    c = base.copy()
    if half == 1:
        for g, w in enumerate((2, 4, 8, 16)):
            c[:, C_INV + g * 16: C_INV + g * 16 + 16] = 1.0 / w
        c[:, C_PM] = 0.0
    else:
        c[:, C_PM] = -30000.0
    return c


def make_in_maps(inputs, npass=4, ncores=NCORES):
    f = lambda a: np.ascontiguousarray(np.asarray(a, dtype=np.float32))
    x_prompt = f(inputs["x_prompt"])
    x_sample = f(inputs["x_sample"])
    cache_k = f(inputs["cache_k"])
    cache_v = f(inputs["cache_v"])
    cache_logf = f(inputs["cache_logf"])
    state_pool = f(inputs["state_pool"])
    cst = _consts(f(inputs["norm_a"]), f(inputs["norm_kv"]), f(inputs["norm_b"]), f(inputs["norm_f"]),
                  f(inputs["scale_a"]), f(inputs["b_f"]))
    shared = {
        "w_in_a": f(inputs["w_in_a"]), "w_grp_a": f(inputs["w_grp_a"]), "w_out_a": f(inputs["w_out_a"]),
        "w_kv": f(inputs["w_kv"]), "w_in_b": f(inputs["w_in_b"]), "w_out_b": f(inputs["w_out_b"]),
    }
    nb = x_prompt.shape[0]
    nsb = x_sample.shape[0]
    HT = npass * NT
    maps = []
    for c in range(ncores):
        b, half = (c // 2) % nb, c % 2
        m = dict(shared)
        m["cst"] = _core_consts(cst, half)
        m["xp"] = np.ascontiguousarray(x_prompt[b, half * HT:(half + 1) * HT])
        m["xh"] = np.ascontiguousarray(x_prompt[b, HT - 32:HT]) if half == 1 else np.zeros((32, D), np.float32)
        sb0 = (2 * c) % nsb
        m["xs"] = np.ascontiguousarray(x_sample[sb0:sb0 + 2].reshape(32, D))
        m["ck"] = np.ascontiguousarray(cache_k[sb0:sb0 + 2].reshape(2, PAST, D))
        m["cv"] = np.ascontiguousarray(cache_v[sb0:sb0 + 2].reshape(2, PAST, D))
        m["clf"] = np.ascontiguousarray(cache_logf[sb0:sb0 + 2])
        m["stp"] = np.ascontiguousarray(state_pool[:, sb0:sb0 + 2])
        maps.append(m)
    return maps


_NC_CACHE = {}


def kernel(**inputs):
    if "nc" not in _NC_CACHE:
        _NC_CACHE["nc"] = build_nc(4, True)[0]
    nc = _NC_CACHE["nc"]
    maps = make_in_maps(inputs, 4, NCORES)
    res = run_bass_kernel_spmd(nc, maps, core_ids=list(range(NCORES)))
    r = res.results
    cat = lambda name, b: np.concatenate([r[2 * b][name], r[2 * b + 1][name]], axis=0)
    y_prompt = np.stack([cat("yp", b) for b in range(4)])
    k_prompt = np.stack([cat("kp", b) for b in range(4)]).reshape(4, 4096, H, DH)
    v_prompt = np.stack([cat("vp", b) for b in range(4)]).reshape(4, 4096, H, DH)
    logf_prompt = np.stack([cat("lfp", b) for b in range(4)])
    pool_prompt = np.stack([r[2 * b + 1]["poolp"] for b in range(4)], axis=1)
    y_sample = np.concatenate([r[c]["ys"].reshape(2, 16, D) for c in range(NCORES)])
    k_sample = np.concatenate([r[c]["ks"].reshape(2, 16, H, DH) for c in range(NCORES)])
    v_sample = np.concatenate([r[c]["vs"].reshape(2, 16, H, DH) for c in range(NCORES)])
    logf_sample = np.concatenate([r[c]["lfs"].reshape(2, 16, H) for c in range(NCORES)])
    pool_sample = np.concatenate([r[c]["pools"] for c in range(NCORES)], axis=1)
    return (y_prompt, y_sample, k_prompt, v_prompt, logf_prompt, pool_prompt,
            k_sample, v_sample, logf_sample, pool_sample)
```
